# Optimizing a Trainium2 kernel written in Bass

```python
import math
import jax, jax.numpy as jnp
from jax import lax
import numpy as np

D_MODEL = 1024
BATCH = 8
SEQ = 2048
DEPTH = 2

N_EVEN = (DEPTH + 1) // 2
N_ODD = DEPTH // 2
NORM_EPS = 1e-6

S5_WIDTH = D_MODEL // 2
S5_GROUP = 16
S5_GROUPS = S5_WIDTH // S5_GROUP
S5_STATE = 64
S5_DT_MIN = 1e-3
S5_DT_MAX = 1e-1

HG_WIDTH = D_MODEL // 2
HG_HEADS = 4
HG_KDIM = HG_WIDTH // HG_HEADS
HG_VDIM = HG_WIDTH // HG_HEADS
HG_CHUNK = 64

IN0_COLS = S5_WIDTH + 2 * HG_HEADS * HG_KDIM + 2 * HG_WIDTH

ATT_HEAD_DIM = 64
ATT_HEADS_PER_GROUP = 8
ATT_BRANCHES = ((128, 1), (512, 4), (2048, 16))
N_BRANCH = len(ATT_BRANCHES)
ATT_BLOCK = 128
ATT_GROUP_WIDTH = ATT_HEADS_PER_GROUP * ATT_HEAD_DIM
IN1_COLS = 3 * N_BRANCH * ATT_GROUP_WIDTH
ROT_DIM = ATT_HEAD_DIM // 4
ROPE_THETA = 500000.0

D_FF = 2816
CONV_W = 3

kernel_name = "hybrid_s5_hgrn2_dilated_convffn"


def _rmsnorm(x, g):
    xf = x.astype(jnp.float32)
    y = xf * lax.rsqrt(jnp.mean(xf * xf, axis=-1, keepdims=True) + NORM_EPS)
    return (y * g.astype(jnp.float32)).astype(x.dtype)


def _cplx_scan_op(e1, e2):
    a1r, a1i, b1r, b1i = e1
    a2r, a2i, b2r, b2i = e2
    ar = a2r * a1r - a2i * a1i
    ai = a2r * a1i + a2i * a1r
    br = a2r * b1r - a2i * b1i + b2r
    bi = a2r * b1i + a2i * b1r + b2i
    return (ar, ai, br, bi)


def _s5_mixer(u, A_re, A_im, log_dt, B_re, B_im, C_re, C_im, Dd, glu_w, glu_b):
    Bsz, L, _ = u.shape
    f32 = jnp.float32
    A_re, A_im = A_re.astype(f32), A_im.astype(f32)
    B_re, B_im = B_re.astype(f32), B_im.astype(f32)
    C_re, C_im = C_re.astype(f32), C_im.astype(f32)
    ug = u.astype(f32).reshape(Bsz, L, S5_GROUPS, S5_GROUP)
    dt = jnp.exp(log_dt.astype(f32))[:, None]
    mag = jnp.exp(A_re * dt)
    ab_re = mag * jnp.cos(A_im * dt)
    ab_im = mag * jnp.sin(A_im * dt)
    den = A_re * A_re + A_im * A_im
    nr, ni = ab_re - 1.0, ab_im
    c_re = (nr * A_re + ni * A_im) / den
    c_im = (ni * A_re - nr * A_im) / den
    Bb_re = c_re[..., None] * B_re - c_im[..., None] * B_im
    Bb_im = c_re[..., None] * B_im + c_im[..., None] * B_re
    bu_re = jnp.einsum('gpc,blgc->blgp', Bb_re, ug)
    bu_im = jnp.einsum('gpc,blgc->blgp', Bb_im, ug)
    a_re = jnp.broadcast_to(ab_re[None, None], (1, L, S5_GROUPS, S5_STATE))
    a_im = jnp.broadcast_to(ab_im[None, None], (1, L, S5_GROUPS, S5_STATE))
    _, _, x_re, x_im = lax.associative_scan(_cplx_scan_op, (a_re, a_im, bu_re, bu_im), axis=1)
    y = (jnp.einsum('gcp,blgp->blgc', C_re, x_re)
         - jnp.einsum('gcp,blgp->blgc', C_im, x_im)
         + Dd.astype(f32) * ug)
    z = jax.nn.gelu(y.reshape(Bsz, L, S5_WIDTH))
    return z * jax.nn.sigmoid(z @ glu_w.astype(f32) + glu_b.astype(f32))


def _hgrn2_mixer(xq, xf, xi, xg, lb, norm_g):
    Bsz, L, _ = xq.shape
    f32 = jnp.float32
    nc = L // HG_CHUNK
    q = jax.nn.silu(xq.astype(f32)).reshape(Bsz, L, HG_HEADS, HG_KDIM)
    f = lb + (1.0 - lb) * jax.nn.sigmoid(xf.astype(f32))
    k = (1.0 - f).reshape(Bsz, L, HG_HEADS, HG_KDIM)
    logf = jnp.log(f).reshape(Bsz, L, HG_HEADS, HG_KDIM)
    v = xi.astype(f32).reshape(Bsz, L, HG_HEADS, HG_VDIM)

    def to_chunks(t):
        return t.reshape(Bsz, nc, HG_CHUNK, HG_HEADS, -1).transpose(1, 0, 3, 2, 4)

    causal = jnp.tril(jnp.ones((HG_CHUNK, HG_CHUNK), dtype=bool))[None, None, :, :, None]

    def step(S, inp):
        qc, kc, gc, vc = inp
        b = jnp.cumsum(gc, axis=2)
        o_inter = jnp.einsum('bhtk,bhkv->bhtv', qc * jnp.exp(b), S)
        diff = b[:, :, :, None, :] - b[:, :, None, :, :]
        decay = jnp.exp(jnp.where(causal, diff, -jnp.inf))
        att = jnp.einsum('bhtk,bhsk,bhtsk->bhts', qc, kc, decay)
        o_intra = jnp.einsum('bhts,bhsv->bhtv', att, vc)
        b_last = b[:, :, -1, :]
        S_new = (jnp.exp(b_last)[..., None] * S
                 + jnp.einsum('bhsk,bhsv->bhkv', kc * jnp.exp(b_last[:, :, None, :] - b), vc))
        return S_new, o_inter + o_intra

    S0 = jnp.zeros((Bsz, HG_HEADS, HG_KDIM, HG_VDIM), f32)
    _, o = lax.scan(step, S0, (to_chunks(q), to_chunks(k), to_chunks(logf), to_chunks(v)))
    o = o.transpose(1, 0, 3, 2, 4).reshape(Bsz, L, HG_HEADS, HG_VDIM)
    o = o * lax.rsqrt(jnp.mean(o * o, axis=-1, keepdims=True) + NORM_EPS)
    o = o * norm_g.astype(f32).reshape(HG_HEADS, HG_VDIM)
    return o.reshape(Bsz, L, HG_WIDTH) * jax.nn.silu(xg.astype(f32))


def _partial_rotary(t, positions):
    half = ROT_DIM // 2
    inv_freq = ROPE_THETA ** (-jnp.arange(half, dtype=jnp.float32) * 2.0 / ROT_DIM)
    ang = positions.astype(jnp.float32)[..., None] * inv_freq
    cos = jnp.cos(ang)[:, :, None, :]
    sin = jnp.sin(ang)[:, :, None, :]
    x1 = t[..., :half]
    x2 = t[..., half:ROT_DIM]
    return jnp.concatenate([x1 * cos - x2 * sin, x2 * cos + x1 * sin, t[..., ROT_DIM:]], axis=-1)


def _dilated_branch(q, k, v, dil, steps):
    Bsz, L, H, E = q.shape
    M = L // dil
    nb = -(-M // ATT_BLOCK)
    Mp = nb * ATT_BLOCK

    def to_blocks(t):
        t = t.reshape(Bsz, M, dil, H, E).transpose(0, 2, 3, 1, 4)
        t = jnp.pad(t, ((0, 0), (0, 0), (0, 0), (0, Mp - M), (0, 0)))
        return t.reshape(Bsz, dil, H, nb, ATT_BLOCK, E)

    def with_prev(t):
        prev = jnp.pad(t[:, :, :, :-1], ((0, 0), (0, 0), (0, 0), (1, 0), (0, 0), (0, 0)))
        return jnp.concatenate([prev, t], axis=-2)

    qb = to_blocks(q) * (E ** -0.5)
    kc = with_prev(to_blocks(k))
    vc = with_prev(to_blocks(v))
    s = jnp.einsum('bdhnqe,bdhnke->bdhnqk', qb, kc)
    qi = jnp.arange(ATT_BLOCK)[:, None] + ATT_BLOCK
    kj = jnp.arange(2 * ATT_BLOCK)[None, :]
    back = qi - kj
    in_range = (jnp.arange(nb)[:, None, None] * ATT_BLOCK - ATT_BLOCK + kj[None]) >= 0
    valid = (back >= 0) & (back <= steps) & in_range
    s = jnp.where(valid, s, -jnp.inf)
    m = jnp.max(s, axis=-1, keepdims=True)
    p = jnp.exp(s - m)
    den = jnp.sum(p, axis=-1, keepdims=True)
    o = jnp.einsum('bdhnqk,bdhnke->bdhnqe', p, vc) / den
    lse = (m + jnp.log(den))[..., 0]
    o = o.reshape(Bsz, dil, H, Mp, E)[:, :, :, :M].transpose(0, 3, 1, 2, 4).reshape(Bsz, L, H, E)
    lse = lse.reshape(Bsz, dil, H, Mp)[..., :M].transpose(0, 3, 1, 2).reshape(Bsz, L, H)
    return o, lse


def _dilated_attention(h, positions, w_qkv, w_o):
    Bsz, L, _ = h.shape
    f32 = jnp.float32
    qkv = (h @ w_qkv).astype(f32).reshape(Bsz, L, 3, N_BRANCH * ATT_HEADS_PER_GROUP, ATT_HEAD_DIM)
    q = _partial_rotary(qkv[:, :, 0], positions)
    k = _partial_rotary(qkv[:, :, 1], positions)
    v = qkv[:, :, 2]
    outs, lses = [], []
    for g, (win, dil) in enumerate(ATT_BRANCHES):
        sl = slice(g * ATT_HEADS_PER_GROUP, (g + 1) * ATT_HEADS_PER_GROUP)
        o_g, lse_g = _dilated_branch(q[:, :, sl], k[:, :, sl], v[:, :, sl], dil, win // dil)
        outs.append(o_g)
        lses.append(lse_g)
    alpha = jax.nn.softmax(jnp.stack(lses, axis=0), axis=0)
    o = jnp.sum(alpha[..., None] * jnp.stack(outs, axis=0), axis=0)
    return o.reshape(Bsz, L, ATT_GROUP_WIDTH).astype(h.dtype) @ w_o


def _conv_ffn(h, w_in, conv_w, conv_b, w_out):
    hu = h @ w_in
    C = hu.shape[-1]
    hu = lax.conv_general_dilated(
        hu, conv_w.astype(hu.dtype)[:, None, :], window_strides=(1,),
        padding=[(CONV_W - 1, 0)], dimension_numbers=('NWC', 'WIO', 'NWC'),
        feature_group_count=C) + conv_b.astype(hu.dtype)
    a, b = hu[..., :D_FF], hu[..., D_FF:]
    return (jax.nn.silu(a) * b) @ w_out


def setup_inputs(seed: int = 0) -> dict:
    key = jax.random.key(seed)
    ks = jax.random.split(key, 24)
    f32 = jnp.float32

    def nrm(k, shape, scale):
        return jax.random.normal(k, shape, f32) * scale

    mix_width0 = S5_WIDTH + HG_WIDTH
    n_idx = jnp.arange(S5_STATE, dtype=f32)
    return {
        "x": nrm(ks[0], (BATCH, SEQ, D_MODEL), 1.0),
        "positions": jnp.broadcast_to(jnp.arange(SEQ, dtype=jnp.int32), (BATCH, SEQ)),
        "norm_mix": 1.0 + nrm(ks[1], (DEPTH, D_MODEL), 0.02),
        "norm_ffn": 1.0 + nrm(ks[2], (DEPTH, D_MODEL), 0.02),
        "norm_final": 1.0 + nrm(ks[3], (D_MODEL,), 0.02),
        "mix_w_in": nrm(ks[4], (N_EVEN, D_MODEL, IN0_COLS), D_MODEL ** -0.5),
        "mix_w_out": nrm(ks[5], (N_EVEN, mix_width0, D_MODEL), mix_width0 ** -0.5),
        "s5_A_re": -0.5 + nrm(ks[6], (N_EVEN, S5_GROUPS, S5_STATE), 0.01),
        "s5_A_im": math.pi * n_idx + nrm(ks[7], (N_EVEN, S5_GROUPS, S5_STATE), 0.01),
        "s5_log_dt": jax.random.uniform(ks[8], (N_EVEN, S5_GROUPS), f32,
                                        math.log(S5_DT_MIN), math.log(S5_DT_MAX)),
        "s5_B_re": nrm(ks[9], (N_EVEN, S5_GROUPS, S5_STATE, S5_GROUP), (2 * S5_GROUP) ** -0.5),
        "s5_B_im": nrm(ks[10], (N_EVEN, S5_GROUPS, S5_STATE, S5_GROUP), (2 * S5_GROUP) ** -0.5),
        "s5_C_re": nrm(ks[11], (N_EVEN, S5_GROUPS, S5_GROUP, S5_STATE), (2 * S5_STATE) ** -0.5),
        "s5_C_im": nrm(ks[12], (N_EVEN, S5_GROUPS, S5_GROUP, S5_STATE), (2 * S5_STATE) ** -0.5),
        "s5_D": nrm(ks[13], (N_EVEN, S5_GROUPS, S5_GROUP), 1.0),
        "s5_glu_w": nrm(ks[14], (N_EVEN, S5_WIDTH, S5_WIDTH), S5_WIDTH ** -0.5),
        "s5_glu_b": nrm(ks[15], (N_EVEN, S5_WIDTH), 0.01),
        "hgrn_gamma": nrm(ks[16], (N_EVEN + 1, HG_HEADS * HG_KDIM), 0.1),
        "hgrn_norm": 1.0 + nrm(ks[17], (N_EVEN, HG_WIDTH), 0.02),
        "att_w_qkv": nrm(ks[18], (N_ODD, D_MODEL, IN1_COLS), D_MODEL ** -0.5),
        "att_w_o": nrm(ks[19], (N_ODD, ATT_GROUP_WIDTH, D_MODEL), ATT_GROUP_WIDTH ** -0.5),
        "ffn_w_in": nrm(ks[20], (DEPTH, D_MODEL, 2 * D_FF), D_MODEL ** -0.5),
        "ffn_conv_w": nrm(ks[21], (DEPTH, CONV_W, 2 * D_FF), CONV_W ** -0.5),
        "ffn_conv_b": nrm(ks[22], (DEPTH, 2 * D_FF), 0.01),
        "ffn_w_out": nrm(ks[23], (DEPTH, D_FF, D_MODEL), D_FF ** -0.5),
    }


def reference(x, positions, norm_mix, norm_ffn, norm_final, mix_w_in, mix_w_out,
              s5_A_re, s5_A_im, s5_log_dt, s5_B_re, s5_B_im, s5_C_re, s5_C_im, s5_D,
              s5_glu_w, s5_glu_b, hgrn_gamma, hgrn_norm, att_w_qkv, att_w_o,
              ffn_w_in, ffn_conv_w, ffn_conv_b, ffn_w_out):
    lb_all = jnp.cumsum(jax.nn.softmax(hgrn_gamma.astype(jnp.float32), axis=0), axis=0)
    h = x
    c_q = S5_WIDTH
    c_f = c_q + HG_HEADS * HG_KDIM
    c_i = c_f + HG_HEADS * HG_KDIM
    c_g = c_i + HG_WIDTH
    for layer in range(DEPTH):
        hn = _rmsnorm(h, norm_mix[layer])
        j = layer // 2
        if layer % 2 == 0:
            proj = hn @ mix_w_in[j]
            oa = _s5_mixer(proj[..., :c_q], s5_A_re[j], s5_A_im[j], s5_log_dt[j],
                           s5_B_re[j], s5_B_im[j], s5_C_re[j], s5_C_im[j], s5_D[j],
                           s5_glu_w[j], s5_glu_b[j])
            ob = _hgrn2_mixer(proj[..., c_q:c_f], proj[..., c_f:c_i], proj[..., c_i:c_g],
                              proj[..., c_g:], lb_all[j], hgrn_norm[j])
            mix = jnp.concatenate([oa, ob], axis=-1).astype(h.dtype) @ mix_w_out[j]
        else:
            mix = _dilated_attention(hn, positions, att_w_qkv[j], att_w_o[j])
        h = h + mix.astype(h.dtype)
        ff = _conv_ffn(_rmsnorm(h, norm_ffn[layer]), ffn_w_in[layer], ffn_conv_w[layer],
                       ffn_conv_b[layer], ffn_w_out[layer])
        h = h + ff.astype(h.dtype)
    return _rmsnorm(h, norm_final)
```

```python
import numpy as np
from contextlib import ExitStack
import concourse.bass as bass
import concourse.mybir as mybir
from concourse.bass_utils import run_bass_kernel_spmd

F32 = mybir.dt.float32
I32 = mybir.dt.int32
AF = mybir.ActivationFunctionType
ALU = mybir.AluOpType

ENGS = ("pe", "act", "dve", "pool", "sp")
NDMA_SEM = 8
L = 2048
D = 1024
DFF = 2816
EPS = 1e-6


class _Op:
    __slots__ = ("eng", "fn", "deps", "dma", "sig", "idx", "name", "dj")


class Prog:
    def __init__(self):
        self.ops = []
        self.last_writer = {}
        self.readers = {}

    def op(self, eng, fn, reads=(), writes=(), dma=False, name=""):
        deps = set()
        lw = self.last_writer
        rd = self.readers
        for k in reads:
            w = lw.get(k)
            if w is not None:
                deps.add(w)
        for k in writes:
            w = lw.get(k)
            if w is not None:
                deps.add(w)
            r = rd.get(k)
            if r:
                deps.update(r)
        o = _Op()
        o.eng, o.fn, o.dma, o.name = eng, fn, dma, name
        o.idx = len(self.ops)
        deps.discard(o.idx)
        o.deps = deps
        o.sig = None
        self.ops.append(o)
        for k in reads:
            rd.setdefault(k, []).append(o.idx)
        for k in writes:
            lw[k] = o.idx
            rd[k] = []
        return o.idx

    def _skip(self, od, o):
        return od.eng == "pe" and o.eng == "pe" and not od.dma and not o.dma

    def emit(self, sem_ctx):
        ops = self.ops
        needed = [False] * len(ops)
        for o in ops:
            for d in o.deps:
                if not self._skip(ops[d], o):
                    needed[d] = True
        eng_sem = {}
        cnt = {e: 0 for e in ENGS}
        dcnt = {e: 0 for e in ENGS}
        dma_sems = {}
        per_eng = {e: [] for e in ENGS}
        for o in ops:
            per_eng[o.eng].append(o)
            if o.dma:
                j = dcnt[o.eng]
                dcnt[o.eng] += 1
                key = (o.eng, j % NDMA_SEM)
                if key not in dma_sems:
                    dma_sems[key] = sem_ctx("d%s%d" % key)
                o.sig = (dma_sems[key], 16 * (j // NDMA_SEM + 1), 16)
                o.dj = j
            elif needed[o.idx]:
                if o.eng not in eng_sem:
                    eng_sem[o.eng] = sem_ctx("c" + o.eng)
                cnt[o.eng] += 1
                o.sig = (eng_sem[o.eng], cnt[o.eng], 1)
        self.per_eng = per_eng
        self.dma_lists = {e: [o for o in per_eng[e] if o.dma] for e in ENGS}
        self._deadlock_check()

    def _waits_for(self, o):
        ops = self.ops
        w = []
        for d in o.deps:
            od = ops[d]
            if self._skip(od, o):
                continue
            w.append(od.sig[:2])
        if o.dma and o.dj >= NDMA_SEM:
            w.append(self.dma_lists[o.eng][o.dj - NDMA_SEM].sig[:2])
        return w

    def _deadlock_check(self):
        per_eng = self.per_eng
        semval = {}
        pos = {e: 0 for e in ENGS}
        total = sum(len(v) for v in per_eng.values())
        done = 0
        while done < total:
            prog = False
            for e in ENGS:
                lst = per_eng[e]
                while pos[e] < len(lst):
                    o = lst[pos[e]]
                    if not all(semval.get(id(s), 0) >= v for s, v in self._waits_for(o)):
                        break
                    if o.sig is not None:
                        semval[id(o.sig[0])] = semval.get(id(o.sig[0]), 0) + o.sig[2]
                    pos[e] += 1
                    done += 1
                    prog = True
            if not prog:
                raise RuntimeError("deadlock in op graph")

    def run_engine(self, e, engobj):
        lst = self.per_eng[e]
        seen = {}
        for o in lst:
            for s, v in self._waits_for(o):
                if seen.get(id(s), 0) >= v:
                    continue
                seen[id(s)] = v
                engobj.wait_ge(s, v)
            ins = o.fn(engobj)
            if o.sig is not None:
                ins.then_inc(o.sig[0], o.sig[2])
        last = {}
        for o in self.dma_lists[e]:
            last[id(o.sig[0])] = o.sig[:2]
        for s, v in last.values():
            if seen.get(id(s), 0) < v:
                engobj.wait_ge(s, v)


class V:
    __slots__ = ("t", "sp", "off", "dims", "p0", "np", "F")

    def __init__(self, t, sp, F, off, dims, p0=0, np_=128):
        self.t, self.sp, self.F, self.off, self.dims, self.p0, self.np = t, sp, F, off, dims, p0, np_

    def ap(self):
        return bass.AP(self.t, self.p0 * self.F + self.off, [[self.F, self.np]] + [[s, n] for s, n in self.dims])

    def keys(self):
        lo = self.off
        hi = self.off + sum((n - 1) * s for s, n in self.dims)
        ks = []
        for q in range(self.p0 // 32, (self.p0 + self.np - 1) // 32 + 1):
            for c in range(lo // 512, hi // 512 + 1):
                ks.append((self.sp, q, c))
        return ks


class DR:
    __slots__ = ("a", "k")

    def __init__(self, a, k):
        self.a, self.k = a, list(k)

    def ap(self):
        return self.a

    def keys(self):
        return self.k


def _ks(*vs):
    out = []
    for v in vs:
        if isinstance(v, (V, DR)):
            out.extend(v.keys())
    return out


def _a(v):
    return v.ap() if isinstance(v, (V, DR)) else v


C_IDENT = 0
C_ATTM = 128
C_HGM = 384
C_S5M = 448
C_INVF = 576
C_SGN = 577
C_EPS = 578
C_ONE = 579
C_ONES = 640
C_M0 = 768
C_M1 = 896
NCONST = 1024


def _const_table():
    c = np.zeros((128, NCONST), np.float32)
    c[:, C_IDENT:C_IDENT + 128] = np.eye(128, dtype=np.float32)
    k = np.arange(128)[:, None]
    q = np.arange(128)[None, :]
    c[:, C_ATTM:C_ATTM + 128] = (q >= k)
    c[:, C_ATTM + 128:C_ATTM + 256] = (k >= q)
    s = (np.arange(128) % 64)[:, None]
    t = np.arange(64)[None, :]
    c[:, C_HGM:C_HGM + 64] = (s <= t)
    j = (np.arange(128) // 16)[:, None]
    i = (np.arange(128) // 16)[None, :]
    c[:, C_S5M:C_S5M + 128] = (i >= j)
    e = np.arange(128) % 64
    invf = np.where(e < 16, 500000.0 ** (-(e % 8).astype(np.float64) * 2.0 / 16.0), 0.0)
    c[:, C_INVF] = invf.astype(np.float32)
    c[:, C_SGN] = np.where(e < 8, -1.0, np.where(e < 16, 1.0, 0.0))
    c[:, C_EPS] = EPS
    c[:, C_ONE] = 1.0
    c[:, C_ONES:C_ONES + 128] = 1.0
    c[:, C_M0:C_M0 + 64] = 1.0
    c[:, C_M1 + 64:C_M1 + 128] = 1.0
    return c


A_CONST = 0
A_CV = NCONST
A_MISC = NCONST + 512
A_H = 2048
A_X = A_H + 16384
A_Y = A_X + 16384
ARENA = A_Y + 12288

IN_SPECS = [
    ("x", [L, D], F32), ("positions", [L], I32), ("norm_mix", [2, D], F32), ("norm_ffn", [2, D], F32),
    ("norm_final", [D], F32), ("mix_w_in", [1, D, 2560], F32), ("mix_w_out", [1, D, D], F32),
    ("s5_A_re", [1, 32, 64], F32), ("s5_A_im", [1, 32, 64], F32), ("s5_log_dt", [1, 32], F32),
    ("s5_B_re", [1, 32, 64, 16], F32), ("s5_B_im", [1, 32, 64, 16], F32), ("s5_C_re", [1, 32, 16, 64], F32),
    ("s5_C_im", [1, 32, 16, 64], F32), ("s5_D", [1, 32, 16], F32), ("s5_glu_w", [1, 512, 512], F32),
    ("s5_glu_b", [1, 512], F32), ("hgrn_gamma", [2, 512], F32), ("hgrn_norm", [1, 512], F32),
    ("att_w_qkv", [1, D, 4608], F32), ("att_w_o", [1, 512, D], F32), ("ffn_w_in", [2, D, 2 * DFF], F32),
    ("ffn_conv_w", [2, 3, 2 * DFF], F32), ("ffn_conv_b", [2, 2 * DFF], F32), ("ffn_w_out", [2, DFF, D], F32),
    ("consts", [128, NCONST], F32),
]


class KB:
    def __init__(self, mode="full", dbg=False):
        self.mode = mode
        self.nc = nc = bass.Bass("TRN2", target_bir_lowering=False)
        self.P = Prog()
        self.din = {}
        for n, shp, dt in IN_SPECS:
            self.din[n] = nc.dram_tensor(n, shp, dt, kind="ExternalInput")
        self.dout = nc.dram_tensor("out", [L, D], F32, kind="ExternalOutput")
        sk = "ExternalOutput" if dbg else "Internal"
        self.scr = {}
        for n, shp in [("QK", [24, 128, L]), ("VS", [3, 16, 128, 512]), ("PJ", [20, 128, L]), ("VT", [L, 512]),
                       ("US", [32, 128, 256]), ("YS", [32, 128, 256])]:
            self.scr[n] = nc.dram_tensor("scr_" + n, shp, F32, kind=sk)
        self.dbg = dbg

    def sb(self, off, n, p0=0, np_=128):
        return V(self.ar, "sb", ARENA, off, [(1, n)], p0, np_)

    def sbd(self, off, dims, p0=0, np_=128):
        return V(self.ar, "sb", ARENA, off, dims, p0, np_)

    def psv(self, bank, n=512, off=0, p0=0, np_=128):
        return V(self.ps, "ps", 4096, bank * 512 + off, [(1, n)], p0, np_)

    def psd(self, bank, dims, off=0, p0=0, np_=128):
        return V(self.ps, "ps", 4096, bank * 512 + off, dims, p0, np_)

    def dr(self, name, off, dims, keys):
        t = self.din[name] if name in self.din else (self.scr[name] if name in self.scr else self.dout)
        return DR(bass.AP(t, off, [list(d) for d in dims]), keys)

    def mm(self, out, lhsT, rhs, start=True, stop=True):
        self.P.op("pe", lambda e: e.matmul(out.ap(), lhsT.ap(), rhs.ap(), start=start, stop=stop),
                  reads=_ks(lhsT, rhs), writes=_ks(out))

    def tr(self, out, in_):
        idv = self.sb(A_CONST + C_IDENT, in_.np, 0, in_.np)
        self.P.op("pe", lambda e: e.transpose(out.ap(), in_.ap(), idv.ap()), reads=_ks(in_, idv), writes=_ks(out))

    def act(self, out, in_, func, scale=1.0, bias=0.0, accum=None, eng="act"):
        kw = {}
        if accum is not None:
            kw["accum_out"] = accum.ap()
        self.P.op(eng, lambda e: e.activation(out=out.ap(), in_=in_.ap(), func=func, scale=_a(scale), bias=_a(bias), **kw),
                  reads=_ks(in_, scale, bias), writes=_ks(out, accum))

    def ts(self, eng, out, in0, s1, op0, s2=None, op1=None):
        if op1 is None:
            self.P.op(eng, lambda e: e.tensor_scalar(out=out.ap(), in0=in0.ap(), scalar1=_a(s1), scalar2=None, op0=op0),
                      reads=_ks(in0, s1), writes=_ks(out))
        else:
            self.P.op(eng, lambda e: e.tensor_scalar(out=out.ap(), in0=in0.ap(), scalar1=_a(s1), scalar2=_a(s2), op0=op0, op1=op1),
                      reads=_ks(in0, s1, s2), writes=_ks(out))

    def tt(self, eng, out, in0, in1, op):
        self.P.op(eng, lambda e: e.tensor_tensor(out=out.ap(), in0=in0.ap(), in1=in1.ap(), op=op),
                  reads=_ks(in0, in1), writes=_ks(out))

    def stt(self, out, in0, scalar, in1, op0, op1):
        self.P.op("dve", lambda e: e.scalar_tensor_tensor(out=out.ap(), in0=in0.ap(), scalar=_a(scalar), in1=in1.ap(), op0=op0, op1=op1),
                  reads=_ks(in0, scalar, in1), writes=_ks(out))

    def cp(self, eng, out, in_):
        if eng == "act":
            self.P.op("act", lambda e: e.copy(out=out.ap(), in_=in_.ap()), reads=_ks(in_), writes=_ks(out))
        else:
            self.P.op(eng, lambda e: e.tensor_copy(out=out.ap(), in_=in_.ap()), reads=_ks(in_), writes=_ks(out))

    def memset(self, eng, out, val):
        self.P.op(eng, lambda e: e.memset(out.ap(), val), writes=_ks(out))

    def recip(self, out, in_):
        self.P.op("dve", lambda e: e.reciprocal(out=out.ap(), in_=in_.ap()), reads=_ks(in_), writes=_ks(out))

    def dma(self, q, out, in_):
        self.P.op(q, lambda e: e.dma_start(out=out.ap(), in_=in_.ap()), reads=_ks(in_), writes=_ks(out), dma=True)

    def H(self, kt, t0=0, n=L):
        return self.sb(A_H + kt * L + t0, n)

    def cv(self, name, j=0):
        return self.sb(self.cvoff[name] + j, 1)

    def load_consts(self):
        self.dma("sp", self.sb(A_CONST, NCONST), self.dr("consts", 0, [(NCONST, 128), (1, NCONST)], ["consts"]))
        rows = []
        rows.append(("nm0", "norm_mix", 0, 8)); rows.append(("nm1", "norm_mix", D, 8))
        rows.append(("nf0", "norm_ffn", 0, 8)); rows.append(("nf1", "norm_ffn", D, 8))
        rows.append(("nfin", "norm_final", 0, 8))
        rows.append(("s5D", "s5_D", 0, 4)); rows.append(("glub", "s5_glu_b", 0, 4))
        rows.append(("gam0", "hgrn_gamma", 0, 4)); rows.append(("gam1", "hgrn_gamma", 512, 4))
        rows.append(("hnorm", "hgrn_norm", 0, 4))
        for l in range(2):
            for j in range(3):
                rows.append(("cw%d%d" % (l, j), "ffn_conv_w", (l * 3 + j) * 2 * DFF, 44))
            rows.append(("cb%d" % l, "ffn_conv_b", l * 2 * DFF, 44))
        self.cvoff = {}
        off = A_CV
        groups = []
        cur = []
        n = 0
        for r in rows:
            if n + r[3] > 128:
                groups.append(cur); cur = []; n = 0
            cur.append(r); n += r[3]
        groups.append(cur)
        assert sum(r[3] for r in rows) <= 512
        for gi, grp in enumerate(groups):
            stg = A_Y + (gi % 2) * 128
            p = 0
            for (nm, dn, doff, nr) in grp:
                self.dma("sp", self.sb(stg, 128, p, nr), self.dr(dn, doff, [(128, nr), (1, 128)], [dn]))
                self.cvoff[nm] = off + p
                p += nr
            bank = gi % 2
            self.tr(self.psv(bank, p), self.sb(stg, 128, 0, p))
            self.cp("dve", self.sb(off, p), self.psv(bank, p))
            off += p

    def load_x(self):
        for tt in range(16):
            stg = A_Y + 512 + (tt % 2) * 1024
            self.dma("sp", self.sb(stg, 1024), self.dr("x", tt * 128 * D, [(D, 128), (1, D)], ["x"]))
            for half in range(2):
                bank = (tt * 2 + half) % 4
                for q in range(4):
                    self.tr(self.psv(bank, 128, q * 128), self.sb(stg + (half * 4 + q) * 128, 128))
                eng = "dve" if half == 0 else "act"
                self.cp(eng, self.sbd(A_H + half * 4 * L + tt * 128, [(L, 4), (1, 128)]), self.psd(bank, [(128, 4), (1, 128)]))

    def rmsnorm(self, gname, t0, n, dst_off, dst_stride, tmp_off, bank):
        sq = [tmp_off, tmp_off + 512]
        rs = tmp_off + 1024
        acc = self.psv(bank, n)
        for kt in range(8):
            s = self.sb(sq[kt % 2], n)
            self.act(s, self.H(kt, t0, n), AF.Square)
            self.mm(acc, self.sb(A_CONST + C_ONES, 128), s, start=(kt == 0), stop=(kt == 7))
        rsv = self.sb(rs, n)
        self.act(rsv, acc, AF.Sqrt, scale=1.0 / D, bias=self.sb(A_CONST + C_EPS, 1))
        self.recip(rsv, rsv)
        for kt in range(8):
            self.stt(self.sb(dst_off + kt * dst_stride, n), self.H(kt, t0, n), self.cv(gname, kt), rsv, ALU.mult, ALU.mult)

    def load_w(self, dst_off, name, base, row_stride, c0, ncols, nkt=8, q="sp"):
        self.dma(q, self.sbd(dst_off, [(ncols, nkt), (1, ncols)]),
                 self.dr(name, base + c0, [(row_stride, 128), (128 * row_stride, nkt), (1, ncols)], [name]))

    def ffn(self, layer):
        nfn = "nf%d" % layer
        XN = A_X
        G = A_X + 4096
        WI = [A_Y + i * 1024 for i in range(3)]
        WO = [A_Y + 3072 + i * 2816 for i in range(2)]
        RAWS = [A_Y + 8704, A_Y + 9728]
        CVA = A_Y + 10752
        CVB = CVA + 512
        SA = CVB + 512
        HALO = A_MISC
        wi_base = layer * D * 2 * DFF
        wo_base = layer * DFF * D
        wcnt = 0
        for b in range(4):
            t0 = b * 512
            self.rmsnorm(nfn, t0, 512, XN, 512, CVA, 7)
            for ft in range(22):
                cvs = []
                for ab in range(2):
                    tile = ft + ab * 22
                    wb = WI[wcnt % 3]; wcnt += 1
                    self.load_w(wb, "ffn_w_in", wi_base, 2 * DFF, tile * 128, 128)
                    bank = (ft * 2 + ab) % 4
                    acc = self.psv(bank)
                    for kt in range(8):
                        self.mm(acc, self.sb(wb + kt * 128, 128), self.sb(XN + kt * 512, 512), start=(kt == 0), stop=(kt == 7))
                    raw = RAWS[ab]
                    if b == 0:
                        self.memset("pool", self.sb(raw, 2), 0.0)
                    else:
                        self.cp("pool", self.sb(raw, 2), self.sb(HALO + tile * 2, 2))
                    self.cp("act", self.sb(raw + 2, 512), acc)
                    cvt = self.sb(CVA if ab == 0 else CVB, 512)
                    self.act(cvt, acc, AF.Identity, scale=self.cv("cw%d2" % layer, tile), bias=self.cv("cb%d" % layer, tile))
                    self.stt(cvt, self.sb(raw + 1, 512), self.cv("cw%d1" % layer, tile), cvt, ALU.mult, ALU.add)
                    self.stt(cvt, self.sb(raw, 512), self.cv("cw%d0" % layer, tile), cvt, ALU.mult, ALU.add)
                    if b < 3:
                        self.cp("pool", self.sb(HALO + tile * 2, 2), self.sb(raw + 512, 2))
                    cvs.append(cvt)
                sa = self.sb(SA, 512)
                self.act(sa, cvs[0], AF.Silu)
                self.tt("pool", self.sb(G + ft * 512, 512), sa, cvs[1], ALU.mult)
            for dt in range(8):
                wb = WO[dt % 2]
                self.load_w(wb, "ffn_w_out", wo_base, D, dt * 128, 128, nkt=22)
                acc = self.psv(4 + dt % 2)
                for ft in range(22):
                    self.mm(acc, self.sb(wb + ft * 128, 128), self.sb(G + ft * 512, 512), start=(ft == 0), stop=(ft == 21))
                hv = self.H(dt, t0, 512)
                self.tt("dve", hv, acc, hv, ALU.add)

    def layer0_mixer(self):
        Y = A_Y
        XN = A_X
        for b in range(4):
            self.rmsnorm("nm0", b * 512, 512, XN + b * 512, L, Y + 8192, 7)
        LB = A_MISC + 128
        OML = A_MISC + 132
        self.tt("dve", self.sb(LB, 4), self.sb(self.cvoff["gam0"], 4), self.sb(self.cvoff["gam1"], 4), ALU.subtract)
        self.act(self.sb(LB, 4), self.sb(LB, 4), AF.Sigmoid)
        self.ts("dve", self.sb(OML, 4), self.sb(LB, 4), -1.0, ALU.mult, 1.0, ALU.add)
        WA = [Y, Y + 1024]
        TO = [Y + 2048, Y + 2560]
        cnt = 0
        for tile in list(range(0, 12)) + list(range(16, 20)):
            wa = WA[cnt % 2]
            self.load_w(wa, "mix_w_in", 0, 2560, tile * 128, 128)
            for b in range(4):
                acc = self.psv(cnt % 2 * 2 + b % 2)
                for kt in range(8):
                    self.mm(acc, self.sb(wa + kt * 128, 128), self.sb(XN + kt * L + b * 512, 512), start=(kt == 0), stop=(kt == 7))
                to = self.sb(TO[b % 2], 512)
                if tile < 4:
                    self.cp("dve", to, acc)
                elif tile < 8 or tile >= 16:
                    self.act(to, acc, AF.Silu)
                else:
                    self.act(to, acc, AF.Sigmoid)
                    self.ts("dve", to, to, self.sb(OML + tile - 8, 1), ALU.mult, self.sb(LB + tile - 8, 1), ALU.add)
                self.dma("pool", self.dr("PJ", tile * 128 * L + b * 512, [(L, 128), (1, 512)], [("PJ", tile, b)]), to)
            cnt += 1
        WV = Y + 4096
        self.load_w(WV, "mix_w_in", 0, 2560, 1536, 512)
        for tt_ in range(16):
            pv = self.psv(4 + tt_ % 2)
            for kt in range(8):
                self.mm(pv, self.sb(XN + kt * L + tt_ * 128, 128), self.sb(WV + kt * 512, 512), start=(kt == 0), stop=(kt == 7))
            vo = self.sb(TO[tt_ % 2], 512)
            self.cp("act" if tt_ % 2 else "dve", vo, pv)
            self.dma("pool", self.dr("VT", tt_ * 128 * 512, [(512, 128), (1, 512)], [("VT", tt_)]), vo)
        if self.mode != "l0b":
            self.hgrn2()
        if self.mode != "l0a":
            self.s5()
        WO = [Y, Y + 1024]
        for dt in range(8):
            wo = WO[dt % 2]
            self.load_w(wo, "mix_w_out", 0, D, dt * 128, 128)
            for b in range(4):
                acc = self.psv((dt * 4 + b) % 4)
                for kt in range(8):
                    self.mm(acc, self.sb(wo + kt * 128, 128), self.sb(A_X + kt * L + b * 512, 512), start=(kt == 0), stop=(kt == 7))
                hv = self.H(dt, b * 512, 512)
                self.tt("dve", hv, acc, hv, ALU.add)
        if self.dbg:
            for kt in range(8):
                self.dma("pool", self.dr("QK", kt * 128 * L, [(L, 128), (1, L)], [("QKdbg", kt)]), self.sb(A_X + kt * L, L))

    def pj_load(self, dst, tile):
        self.dma("sp", self.sb(dst, L), self.dr("PJ", tile * 128 * L, [(L, 128), (1, L)], [("PJ", tile, b) for b in range(4)]))

    def hgrn2(self):
        Y = A_Y
        RM = A_X
        QT = A_X + 2048
        KT = A_X + 4096
        BB = A_X + 6144
        ENB = Y
        KTOK = Y + 2048
        VTOK = Y + 4096
        OH = Y + 6144
        SG = Y + 8192
        SS = [Y + 10240, Y + 10368]
        ATT = Y + 10496
        TMP = Y + 10752
        self.memset("pool", self.sb(RM, L), 1.0)
        self.memset("pool", self.sbd(RM, [(64, 32)]), 0.0)
        for h in range(4):
            self.pj_load(QT, 4 + h)
            self.pj_load(BB, 8 + h)
            self.pj_load(SG, 16 + h)
            self.dma("sp", self.sbd(VTOK, [(128, 16), (1, 128)]),
                     self.dr("VT", h * 128, [(512, 128), (128 * 512, 16), (1, 128)], [("VT", t) for t in range(16)]))
            f = self.sb(BB, L)
            kk = self.sb(KT, L)
            self.ts("dve", kk, f, -1.0, ALU.mult, 1.0, ALU.add)
            self.act(f, f, AF.Ln)
            self.P.op("dve", lambda e, o=self.sb(BB, L), d0=self.sb(RM, L): e.tensor_tensor_scan(
                out=o.ap(), data0=d0.ap(), data1=o.ap(), initial=0.0, op0=ALU.mult, op1=ALU.add),
                reads=_ks(self.sb(BB, L), self.sb(RM, L)), writes=_ks(self.sb(BB, L)))
            self.act(self.sb(ENB, L), f, AF.Exp, scale=-1.0)
            self.act(f, f, AF.Exp)
            self.tt("dve", self.sb(QT, L), self.sb(QT, L), f, ALU.mult)
            self.tt("pool", kk, kk, self.sb(ENB, L), ALU.mult)
            for t_ in range(16):
                pb = self.psv(t_ % 2, 128)
                self.tr(pb, self.sb(KT + t_ * 128, 128))
                self.cp("act" if t_ % 2 else "dve", self.sb(KTOK + t_ * 128, 128), pb)
            for c in range(32):
                t_, half = c // 2, c % 2
                p0 = 64 * half
                c0 = c * 64
                pa = self.psv(2 + c % 2, 64, 0, p0, 64)
                self.mm(pa, self.sb(KT + c0, 64), self.sb(QT + c0, 64))
                att = self.sb(ATT + (c % 2) * 64, 64, p0, 64)
                self.tt("dve", att, pa, self.sb(A_CONST + C_HGM, 64, p0, 64), ALU.mult)
                po = self.psv(4 + c % 2, 64)
                vt = self.sb(VTOK + t_ * 128, 128, p0, 64)
                S_old = self.sb(SS[(c + 1) % 2], 128)
                S_new = self.sb(SS[c % 2], 128)
                self.mm(po, vt, att, start=True, stop=(c == 0))
                if c > 0:
                    self.mm(po, S_old, self.sb(QT + c0, 64), start=False, stop=True)
                self.cp("act", self.sb(OH + c0, 64), po)
                if c < 31:
                    pS = self.psv(6 + c % 2, 128)
                    self.mm(pS, self.sb(KTOK + t_ * 128, 128, p0, 64), vt, start=True, stop=(c == 0))
                    if c > 0:
                        self.mm(pS, self.sb(A_CONST + C_IDENT, 128), S_old, start=False, stop=True)
                    self.ts("dve", S_new, pS, self.sb(BB + c0 + 63, 1), ALU.mult)
            for b in range(4):
                sq = self.sb(TMP, 512)
                self.act(sq, self.sb(OH + b * 512, 512), AF.Square)
                pn = self.psv(b % 2)
                self.mm(pn, self.sb(A_CONST + C_ONES, 128), sq)
                rs = self.sb(TMP + 512, 512)
                self.act(rs, pn, AF.Sqrt, scale=1.0 / 128, bias=self.sb(A_CONST + C_EPS, 1))
                self.recip(rs, rs)
                ob = self.sb(A_X + (4 + h) * L + b * 512, 512)
                self.stt(ob, self.sb(OH + b * 512, 512), self.cv("hnorm", h), rs, ALU.mult, ALU.mult)
                self.tt("pool", ob, ob, self.sb(SG + b * 512, 512), ALU.mult)

    def s5(self):
        Y = A_Y
        UT = Y
        PP = [(Y + 2048, Y + 4096), (Y + 6144, Y + 8192)]
        SM = Y + 10240
        EB = [SM, SM + 128]
        BD = [SM + 256, SM + 384]
        EC = [SM + 512, SM + 640]
        BT = SM + 768
        SC = SM + 896
        GT = SM + 1152
        it16 = V(self.it, "it", 2048, 0, [(1, 16)])

        def g(i):
            return self.sb(GT + 16 * i, 16)
        ARE, AIM, DT, AR, AI, CRE, CIM, T0, T1, T2 = [g(i) for i in range(10)]
        PWB = GT + 160

        def pw(k, j):
            return self.sb(PWB + (k * 3 + j) * 16, 16)
        for name, dst in (("s5_A_re", ARE), ("s5_A_im", AIM)):
            st = self.sb(SC, 128, 0, 16)
            self.dma("sp", st, self.dr(name, 0, [(128, 16), (1, 128)], [name]))
            pb = self.psv(0, 16)
            self.tr(pb, st)
            self.cp("dve", dst, pb)
        ld = self.sb(BT, 32)
        self.dma("sp", ld, self.dr("s5_log_dt", 0, [(0, 128), (1, 32)], ["s5_log_dt"]))
        self.cp("dve", self.sb(GT + 32, 16, 0, 64), self.sbd(BT, [(2, 16)], 0, 64))
        self.cp("dve", self.sb(GT + 32, 16, 64, 64), self.sbd(BT + 1, [(2, 16)], 64, 64))
        pb = DT
        self.act(DT, pb, AF.Exp)
        self.tt("dve", T0, ARE, DT, ALU.mult)
        self.act(T0, T0, AF.Exp)
        self.tt("dve", T1, AIM, DT, ALU.mult)
        self.ts("dve", T1, T1, 1.0 / (2 * np.pi), ALU.mult)
        for dst, shift in ((AI, 0.0), (AR, 0.25)):
            if shift:
                self.ts("dve", T1, T1, shift, ALU.add)
            self.cp("dve", it16, T1)
            self.cp("dve", T2, it16)
            self.tt("dve", T2, T1, T2, ALU.subtract)
            self.act(dst, T2, AF.Sin, scale=float(2 * np.pi))
        self.tt("dve", AR, AR, T0, ALU.mult)
        self.tt("dve", AI, AI, T0, ALU.mult)
        self.tt("dve", T0, ARE, ARE, ALU.mult)
        self.tt("dve", T1, AIM, AIM, ALU.mult)
        self.tt("dve", T0, T0, T1, ALU.add)
        self.recip(T0, T0)
        self.ts("dve", T1, AR, -1.0, ALU.add)
        self.tt("dve", CRE, T1, ARE, ALU.mult)
        self.tt("dve", T2, AI, AIM, ALU.mult)
        self.tt("dve", CRE, CRE, T2, ALU.add)
        self.tt("dve", CRE, CRE, T0, ALU.mult)
        self.tt("dve", CIM, AI, ARE, ALU.mult)
        self.tt("dve", T2, T1, AIM, ALU.mult)
        self.tt("dve", CIM, CIM, T2, ALU.subtract)
        self.tt("dve", CIM, CIM, T0, ALU.mult)
        self.cp("dve", pw(0, 0), AR)
        self.cp("dve", pw(0, 1), AI)
        for k in range(11):
            if k > 0:
                self.tt("dve", T0, pw(k - 1, 0), pw(k - 1, 0), ALU.mult)
                self.tt("dve", T1, pw(k - 1, 1), pw(k - 1, 1), ALU.mult)
                self.tt("dve", pw(k, 0), T0, T1, ALU.subtract)
                self.tt("dve", T0, pw(k - 1, 0), pw(k - 1, 1), ALU.mult)
                self.ts("dve", pw(k, 1), T0, 2.0, ALU.mult)
            self.ts("dve", pw(k, 2), pw(k, 1), -1.0, ALU.mult)
        for q in range(16):
            ct, gq = q // 4, q % 4
            if gq == 0:
                self.pj_load(UT, ct)
                for b in range(4):
                    self.ts("pool", self.sb(A_X + ct * L + b * 512, 512), self.sb(UT + b * 512, 512), self.cv("s5D", ct), ALU.mult)
            btr, bti, bbr, bbi, btmp = [self.sb(BT + 16 * i, 16) for i in range(5)]
            self.dma("sp", btr, self.dr("s5_B_re", q * 2048, [(16, 128), (1, 16)], ["s5_B_re"]))
            self.dma("sp", bti, self.dr("s5_B_im", q * 2048, [(16, 128), (1, 16)], ["s5_B_im"]))
            cre, cim = self.sb(GT + 16 * 5 + q, 1), self.sb(GT + 16 * 6 + q, 1)
            self.ts("dve", btmp, bti, cim, ALU.mult)
            self.stt(bbr, btr, cre, btmp, ALU.mult, ALU.subtract)
            self.ts("dve", btmp, btr, cim, ALU.mult)
            self.stt(bbi, bti, cre, btmp, ALU.mult, ALU.add)
            for part, bb in ((0, bbr), (1, bbi)):
                self.memset("pool", self.sb(EB[part], 128), 0.0)
                for g2 in range(2):
                    self.cp("pool", self.sb(EB[part] + 32 * gq + 16 * g2, 16, 64 * g2, 64), self.sb(BT + 16 * (2 + part), 16, 64 * g2, 64))
                pb = self.psv(2 + part, 128)
                self.tr(pb, self.sb(EB[part], 128))
                self.cp("act", self.sb(BD[part], 128), pb)
            for part, name in ((0, "s5_C_re"), (1, "s5_C_im")):
                st = self.sb(SC + 128 * part, 128, 0, 16)
                self.dma("sp", self.sbd(SC + 128 * part, [(64, 2), (1, 64)], 0, 16), self.dr(name, q * 2048, [(64, 16), (1024, 2), (1, 64)], [name]))
                pb = self.psv(4 + part, 16)
                self.tr(pb, st)
                self.memset("pool", self.sb(EC[part], 128), 0.0)
                for g2 in range(2):
                    dst = self.sb(EC[part] + 32 * gq + 16 * g2, 16, 64 * g2, 64)
                    src = self.psv(4 + part, 16, 0, 64 * g2, 64)
                    if part == 0:
                        self.cp("dve", dst, src)
                    else:
                        self.ts("dve", dst, src, -1.0, ALU.mult)
            cur = 0
            for part in range(2):
                for b in range(4):
                    pb = self.psv(6 + b % 2)
                    self.mm(pb, self.sb(BD[part], 128), self.sb(UT + b * 512, 512))
                    self.cp("act", self.sb(PP[0][part] + b * 512, 512), pb)
            for k in range(11):
                sft = 1 << k
                n = L - sft
                a_re, a_im = PP[cur]
                b_re, b_im = PP[1 - cur]
                sar = self.sb(PWB + (k * 3 + 0) * 16 + q, 1)
                sai = self.sb(PWB + (k * 3 + 1) * 16 + q, 1)
                snai = self.sb(PWB + (k * 3 + 2) * 16 + q, 1)
                self.stt(self.sb(b_re + sft, n), self.sb(a_re, n), sar, self.sb(a_re + sft, n), ALU.mult, ALU.add)
                self.stt(self.sb(b_re + sft, n), self.sb(a_im, n), snai, self.sb(b_re + sft, n), ALU.mult, ALU.add)
                self.stt(self.sb(b_im + sft, n), self.sb(a_im, n), sar, self.sb(a_im + sft, n), ALU.mult, ALU.add)
                self.stt(self.sb(b_im + sft, n), self.sb(a_re, n), sai, self.sb(b_im + sft, n), ALU.mult, ALU.add)
                self.cp("pool", self.sb(b_re, sft), self.sb(a_re, sft))
                self.cp("pool", self.sb(b_im, sft), self.sb(a_im, sft))
                cur = 1 - cur
            x_re, x_im = PP[cur]
            for b in range(4):
                pb = self.psv(b % 2)
                self.mm(pb, self.sb(EC[0], 128), self.sb(x_re + b * 512, 512), start=True, stop=False)
                self.mm(pb, self.sb(EC[1], 128), self.sb(x_im + b * 512, 512), start=False, stop=True)
                yv = self.sb(A_X + ct * L + b * 512, 512)
                self.tt("dve", yv, pb, yv, ALU.add)
        TG = Y
        for ct in range(4):
            yv = self.sb(A_X + ct * L, L)
            t = self.sb(TG + (ct % 2) * 2048, L)
            self.act(t, yv, AF.Square)
            self.ts("dve", t, t, 0.044715, ALU.mult, 1.0, ALU.add)
            self.tt("pool", t, t, yv, ALU.mult)
            self.act(t, t, AF.Sigmoid, scale=1.5957691216057308)
            self.tt("dve", yv, yv, t, ALU.mult)
        GW = SM
        GG = Y
        for ot in range(4):
            self.dma("sp", self.sbd(GW, [(128, 4), (1, 128)]), self.dr("s5_glu_w", ot * 128, [(512, 128), (128 * 512, 4), (1, 128)], ["s5_glu_w"]))
            for b in range(4):
                pb = self.psv(2 + b % 2)
                for kt in range(4):
                    self.mm(pb, self.sb(GW + kt * 128, 128), self.sb(A_X + kt * L + b * 512, 512), start=(kt == 0), stop=(kt == 3))
                self.act(self.sb(GG + ot * L + b * 512, 512), pb, AF.Sigmoid, bias=self.cv("glub", ot))
        for ct in range(4):
            yv = self.sb(A_X + ct * L, L)
            self.tt("dve" if ct % 2 else "pool", yv, yv, self.sb(GG + ct * L, L), ALU.mult)

    def rot_tables(self, COS, SIN, R, NF):
        it = V(self.it, "it", 2048, 0, [(1, L)])
        self.dma("sp", it, self.dr("positions", 0, [(0, 128), (1, L)], ["positions"]))
        r = self.sb(R, L)
        nf = self.sb(NF, L)
        self.cp("dve", r, it)
        self.ts("dve", r, r, self.sb(A_CONST + C_INVF, 1), ALU.mult, 1.0 / (2 * np.pi), ALU.mult)
        for dst, shift in ((SIN, 0.0), (COS, 0.25)):
            if shift:
                self.ts("dve", r, r, shift, ALU.add)
            self.cp("dve", it, r)
            self.cp("dve", nf, it)
            self.tt("dve", nf, r, nf, ALU.subtract)
            self.act(self.sb(dst, L), nf, AF.Sin, scale=float(2 * np.pi))
        self.ts("dve", self.sb(SIN, L), self.sb(SIN, L), self.sb(A_CONST + C_SGN, 1), ALU.mult)

    def layer1_mixer(self):
        Y = A_Y
        COS, SIN = Y, Y + 2048
        WA = [Y + 4096, Y + 5120]
        WS = [Y + 6144, Y + 7168]
        T1 = [Y + 8192, Y + 8704]
        T2 = [Y + 9216, Y + 9728]
        QO = [Y + 10240, Y + 10752]
        XN = A_X
        for b in range(4):
            self.rmsnorm("nm1", b * 512, 512, XN + b * 512, L, T1[0], 7)
        self.rot_tables(COS, SIN, WA[0], WS[0])
        for w in WS:
            self.memset("pool", self.sb(w, 1024), 0.0)
        cnt = 0
        for i in range(24):
            wa, ws = WA[i % 2], WS[i % 2]
            self.load_w(wa, "att_w_qkv", 0, 4608, i * 128, 128)
            for lo, hi in ((0, 8), (8, 0)):
                self.cp("pool", self.sbd(ws + lo, [(128, 8), (64, 2), (1, 8)]), self.sbd(wa + hi, [(128, 8), (64, 2), (1, 8)]))
            for b in range(4):
                pa = self.psv(cnt % 2)
                pb = self.psv(2 + cnt % 2)
                for kt in range(8):
                    self.mm(pa, self.sb(wa + kt * 128, 128), self.sb(XN + kt * L + b * 512, 512), start=(kt == 0), stop=(kt == 7))
                for kt in range(8):
                    self.mm(pb, self.sb(ws + kt * 128, 128), self.sb(XN + kt * L + b * 512, 512), start=(kt == 0), stop=(kt == 7))
                t1 = self.sb(T1[cnt % 2], 512)
                t2 = self.sb(T2[cnt % 2], 512)
                qo = self.sb(QO[cnt % 2], 512)
                self.tt("dve", t1, pa, self.sb(COS + b * 512, 512), ALU.mult)
                self.tt("dve", t2, pb, self.sb(SIN + b * 512, 512), ALU.mult)
                self.tt("pool", qo, t1, t2, ALU.add)
                self.dma("pool", self.dr("QK", i * 128 * L + b * 512, [(L, 128), (1, 512)], [("QK", i, b)]), qo)
                cnt += 1
        WV = Y + 4096
        for g in range(3):
            dil = (1, 4, 16)[g]
            nbk = 16 // dil
            self.load_w(WV, "att_w_qkv", 0, 4608, 3072 + g * 512, 512)
            for blk in range(16):
                r, n = blk // nbk, blk % nbk
                base = r + dil * 128 * n
                pv = self.psv(4 + blk % 2)
                for kt in range(8):
                    self.mm(pv, self.sbd(XN + kt * L + base, [(dil, 128)]), self.sb(WV + kt * 512, 512), start=(kt == 0), stop=(kt == 7))
                vo = self.sb(T1[0] + (blk % 2) * 1024, 512)
                self.cp("act" if blk % 2 else "dve", vo, pv)
                self.dma("pool", self.dr("VS", (g * 16 + blk) * 128 * 512, [(512, 128), (1, 512)], [("VS", g, blk)]), vo)
        OT = Y
        PT = [Y + 8192, Y + 8704, Y + 9216]
        RD = Y + 9728
        WO = Y + 11776
        ACC_O = A_X + 12288
        ACC_D = A_X + 14336
        ones64 = self.sb(A_CONST + C_ONES, 64)
        ucnt = 0
        pcnt = 0
        ocnt = 0
        for j in range(4):
            for g in range(3):
                dil = (1, 4, 16)[g]
                nbk = 16 // dil
                base_x = A_X + (ucnt % 2) * 6144
                ucnt += 1
                QT, KT, VB = base_x, base_x + 2048, base_x + 4096
                self.dma("sp", self.sb(QT, L), self.dr("QK", (g * 4 + j) * 128 * L, [(L, 128), (1, L)], [("QK", g * 4 + j, b) for b in range(4)]))
                self.dma("sp", self.sb(KT, L), self.dr("QK", (12 + g * 4 + j) * 128 * L, [(L, 128), (1, L)], [("QK", 12 + g * 4 + j, b) for b in range(4)]))
                self.dma("sp", self.sbd(VB, [(128, 16), (1, 128)]),
                         self.dr("VS", g * 16 * 128 * 512 + j * 128, [(512, 128), (128 * 512, 16), (1, 128)], [("VS", g, blk) for blk in range(16)]))
                for hh in range(2):
                    p0 = 64 * hh
                    for r in range(dil):
                        prev = None
                        for n in range(nbk):
                            blk = r * nbk + n
                            base = r + dil * 128 * n
                            nq = 256 if n + 1 < nbk else 128
                            st = self.psv(pcnt % 3, nq)
                            self.mm(st, self.sbd(KT + base, [(dil, 128)], p0, 64), self.sbd(QT + base, [(dil, nq)], p0, 64))
                            pt = self.sb(PT[pcnt % 3], nq)
                            self.act(pt, st, AF.Exp, scale=0.125)
                            self.tt("pool" if pcnt % 2 else "dve", pt, pt, self.sb(A_CONST + C_ATTM, nq), ALU.mult)
                            po = self.psv(4 + ocnt % 2, 128, 0, p0, 64)
                            pd = self.psv(6 + ocnt % 2, 128, 0, p0, 64)
                            ocnt += 1
                            vcur = self.sb(VB + blk * 128 + hh * 64, 64)
                            if prev is not None:
                                vprev = self.sb(VB + (blk - 1) * 128 + hh * 64, 64)
                                pprev = self.sb(prev + 128, 128)
                                self.mm(po, vprev, pprev, start=True, stop=False)
                                self.mm(po, vcur, self.sb(PT[pcnt % 3], 128), start=False, stop=True)
                                self.mm(pd, ones64, pprev, start=True, stop=False)
                                self.mm(pd, ones64, self.sb(PT[pcnt % 3], 128), start=False, stop=True)
                            else:
                                self.mm(po, vcur, self.sb(PT[pcnt % 3], 128), start=True, stop=True)
                                self.mm(pd, ones64, self.sb(PT[pcnt % 3], 128), start=True, stop=True)
                            ao = self.sbd(ACC_O + base, [(dil, 128)], p0, 64)
                            ad = self.sbd(ACC_D + base, [(dil, 128)], p0, 64)
                            if g == 0:
                                self.cp("act", ao, po)
                                self.cp("act", ad, pd)
                            else:
                                self.tt("dve", ao, po, ao, ALU.add)
                                self.tt("dve", ad, pd, ad, ALU.add)
                            prev = PT[pcnt % 3]
                            pcnt += 1
            rd = self.sb(RD, L)
            self.recip(rd, self.sb(ACC_D, L))
            self.tt("dve", self.sb(OT + j * L, L), self.sb(ACC_O, L), rd, ALU.mult)
        for dt in range(8):
            self.dma("sp", self.sbd(WO, [(128, 4), (1, 128)]), self.dr("att_w_o", dt * 128, [(D, 128), (128 * D, 4), (1, 128)], ["att_w_o"]))
            for b in range(4):
                acc = self.psv((dt * 4 + b) % 4)
                for j in range(4):
                    self.mm(acc, self.sb(WO + j * 128, 128), self.sb(OT + j * L + b * 512, 512), start=(j == 0), stop=(j == 3))
                hv = self.H(dt, b * 512, 512)
                self.tt("dve", hv, acc, hv, ALU.add)

    def final(self):
        XN = A_X
        for b in range(4):
            t0 = b * 512
            self.rmsnorm("nfin", t0, 512, XN, 512, A_Y, 7)
            for ts_ in range(4):
                stg = A_Y + 2048 + (ts_ % 2) * 1024
                for half in range(2):
                    bank = (ts_ * 2 + half) % 4
                    for q in range(4):
                        kt = half * 4 + q
                        self.tr(self.psv(bank, 128, q * 128), self.sb(XN + kt * 512 + ts_ * 128, 128))
                    self.cp("dve" if half == 0 else "act", self.sb(stg + half * 512, 512), self.psv(bank))
                tt = b * 4 + ts_
                self.dma("pool", self.dr("out", tt * 128 * D, [(D, 128), (1, D)], ["out%d" % tt]), self.sb(stg, 1024))

    def build(self):
        nc = self.nc
        with ExitStack() as es:
            self.ar = es.enter_context(nc.sbuf_tensor("arena", [128, ARENA], F32))
            self.ps = es.enter_context(nc.psum_tensor("psum", [128, 4096], F32))
            self.it = es.enter_context(nc.sbuf_tensor("itile", [128, 2048], I32))
            self.load_consts()
            self.load_x()
            m = self.mode
            if m in ("full", "l0", "l0a", "l0b"):
                self.layer0_mixer()
            if m in ("full", "l0", "ffn0") and m not in ("l0a", "l0b"):
                self.ffn(0)
            if m in ("full", "l1", "l1y"):
                self.layer1_mixer()
            if m == "l1y":
                self.memset("pool", self.sb(A_Y, 16), 0.0)
            if m in ("full", "l1", "ffn1", "l1y"):
                self.ffn(1)
            self.final()
            P = self.P
            P.emit(lambda name: es.enter_context(nc.semaphore(name)))
            with nc.Block() as block:
                @block.sync
                def _(e):
                    P.run_engine("sp", e)

                @block.tensor
                def _(e):
                    P.run_engine("pe", e)

                @block.scalar
                def _(e):
                    P.run_engine("act", e)

                @block.vector
                def _(e):
                    P.run_engine("dve", e)

                @block.gpsimd
                def _(e):
                    P.run_engine("pool", e)
        return nc


def kernel(**inputs):
    mode = inputs.pop("_mode", "full")
    dbg = inputs.pop("_dbg", False)
    ncores = inputs.pop("_ncores", 8)
    kb = KB(mode, dbg)
    nc = kb.build()
    consts = _const_table()
    in_maps = []
    for c in range(ncores):
        m = {}
        for n, shp, dt in IN_SPECS:
            if n == "consts":
                m[n] = consts
            elif n == "x":
                m[n] = np.ascontiguousarray(inputs["x"][c], dtype=np.float32)
            elif n == "positions":
                m[n] = np.ascontiguousarray(inputs["positions"][c], dtype=np.int32)
            else:
                m[n] = np.ascontiguousarray(inputs[n])
        in_maps.append(m)
    if inputs.get("_trace"):
        res = run_bass_kernel_spmd(nc, in_maps, core_ids=list(range(ncores)), trace=True)
        print("EXEC_TIME_NS", res.exec_time_ns)
    else:
        res = run_bass_kernel_spmd(nc, in_maps, core_ids=list(range(ncores)))
    if dbg:
        return res.results
    return np.stack([r["out"] for r in res.results], axis=0).astype(np.float32)
```

```python
import numpy as np
from contextlib import ExitStack
import concourse.bass as bass
import concourse.mybir as mybir
from concourse.bass_utils import run_bass_kernel_spmd

F32 = mybir.dt.float32
BF16 = mybir.dt.bfloat16
I32 = mybir.dt.int32
AF = mybir.ActivationFunctionType
ALU = mybir.AluOpType

ENGS = ("pe", "act", "dve", "pool", "sp")
NDMA_SEM = 8
L = 2048
D = 1024
DFF = 2816
EPS = 1e-6


class _Op:
    __slots__ = ("eng", "fn", "deps", "dma", "sig", "idx", "name", "dj")


class Prog:
    def __init__(self):
        self.ops = []
        self.last_writer = {}
        self.readers = {}

    def op(self, eng, fn, reads=(), writes=(), dma=False, name=""):
        deps = set()
        lw = self.last_writer
        rd = self.readers
        for k in reads:
            w = lw.get(k)
            if w is not None:
                deps.add(w)
        for k in writes:
            w = lw.get(k)
            if w is not None:
                deps.add(w)
            r = rd.get(k)
            if r:
                deps.update(r)
        o = _Op()
        o.eng, o.fn, o.dma, o.name = eng, fn, dma, name
        o.idx = len(self.ops)
        deps.discard(o.idx)
        o.deps = deps
        o.sig = None
        self.ops.append(o)
        for k in reads:
            rd.setdefault(k, []).append(o.idx)
        for k in writes:
            lw[k] = o.idx
            rd[k] = []
        return o.idx

    def _skip(self, od, o):
        return od.eng == "pe" and o.eng == "pe" and not od.dma and not o.dma

    def emit(self, sem_ctx):
        ops = self.ops
        needed = [False] * len(ops)
        for o in ops:
            for d in o.deps:
                if not self._skip(ops[d], o):
                    needed[d] = True
        eng_sem = {}
        cnt = {e: 0 for e in ENGS}
        dcnt = {e: 0 for e in ENGS}
        dma_sems = {}
        per_eng = {e: [] for e in ENGS}
        for o in ops:
            per_eng[o.eng].append(o)
            if o.dma:
                j = dcnt[o.eng]
                dcnt[o.eng] += 1
                key = (o.eng, j % NDMA_SEM)
                if key not in dma_sems:
                    dma_sems[key] = sem_ctx("d%s%d" % key)
                o.sig = (dma_sems[key], 16 * (j // NDMA_SEM + 1), 16)
                o.dj = j
            elif needed[o.idx]:
                if o.eng not in eng_sem:
                    eng_sem[o.eng] = sem_ctx("c" + o.eng)
                cnt[o.eng] += 1
                o.sig = (eng_sem[o.eng], cnt[o.eng], 1)
        self.per_eng = per_eng
        self.dma_lists = {e: [o for o in per_eng[e] if o.dma] for e in ENGS}
        self._deadlock_check()

    def _waits_for(self, o):
        ops = self.ops
        w = []
        for d in o.deps:
            od = ops[d]
            if self._skip(od, o):
                continue
            w.append(od.sig[:2])
        if o.dma and o.dj >= NDMA_SEM:
            w.append(self.dma_lists[o.eng][o.dj - NDMA_SEM].sig[:2])
        return w

    def _deadlock_check(self):
        per_eng = self.per_eng
        semval = {}
        pos = {e: 0 for e in ENGS}
        total = sum(len(v) for v in per_eng.values())
        done = 0
        while done < total:
            prog = False
            for e in ENGS:
                lst = per_eng[e]
                while pos[e] < len(lst):
                    o = lst[pos[e]]
                    if not all(semval.get(id(s), 0) >= v for s, v in self._waits_for(o)):
                        break
                    if o.sig is not None:
                        semval[id(o.sig[0])] = semval.get(id(o.sig[0]), 0) + o.sig[2]
                    pos[e] += 1
                    done += 1
                    prog = True
            if not prog:
                raise RuntimeError("deadlock in op graph")

    def run_engine(self, e, engobj):
        lst = self.per_eng[e]
        seen = {}
        for o in lst:
            for s, v in self._waits_for(o):
                if seen.get(id(s), 0) >= v:
                    continue
                seen[id(s)] = v
                engobj.wait_ge(s, v)
            ins = o.fn(engobj)
            if o.sig is not None:
                ins.then_inc(o.sig[0], o.sig[2])
        last = {}
        for o in self.dma_lists[e]:
            last[id(o.sig[0])] = o.sig[:2]
        for s, v in last.values():
            if seen.get(id(s), 0) < v:
                engobj.wait_ge(s, v)


class V:
    __slots__ = ("t", "sp", "off", "dims", "p0", "np", "F", "kd")

    def __init__(self, t, sp, F, off, dims, p0=0, np_=128, kd=1):
        self.t, self.sp, self.F, self.off, self.dims, self.p0, self.np, self.kd = t, sp, F, off, dims, p0, np_, kd

    def ap(self):
        return bass.AP(self.t, self.p0 * self.F + self.off, [[self.F, self.np]] + [[s, n] for s, n in self.dims])

    def keys(self):
        lo = self.off // self.kd
        hi = (self.off + sum((n - 1) * s for s, n in self.dims)) // self.kd
        ks = []
        for q in range(self.p0 // 32, (self.p0 + self.np - 1) // 32 + 1):
            for c in range(lo // 512, hi // 512 + 1):
                ks.append((self.sp, q, c))
        return ks


class DR:
    __slots__ = ("a", "k")

    def __init__(self, a, k):
        self.a, self.k = a, list(k)

    def ap(self):
        return self.a

    def keys(self):
        return self.k


def _ks(*vs):
    out = []
    for v in vs:
        if isinstance(v, (V, DR)):
            out.extend(v.keys())
    return out


def _a(v):
    return v.ap() if isinstance(v, (V, DR)) else v


C_IDENT = 0
C_ATTM = 128
C_HGM = 384
C_S5M = 448
C_INVF = 576
C_SGN = 577
C_EPS = 578
C_ONE = 579
C_ONES = 640
C_M0 = 768
C_M1 = 896
NCONST = 1024


def _const_table():
    c = np.zeros((128, NCONST), np.float32)
    c[:, C_IDENT:C_IDENT + 128] = np.eye(128, dtype=np.float32)
    k = np.arange(128)[:, None]
    q = np.arange(128)[None, :]
    c[:, C_ATTM:C_ATTM + 128] = (q >= k)
    c[:, C_ATTM + 128:C_ATTM + 256] = (k >= q)
    s = (np.arange(128) % 64)[:, None]
    t = np.arange(64)[None, :]
    c[:, C_HGM:C_HGM + 64] = (s <= t)
    j = (np.arange(128) // 16)[:, None]
    i = (np.arange(128) // 16)[None, :]
    c[:, C_S5M:C_S5M + 128] = (i >= j)
    e = np.arange(128) % 64
    invf = np.where(e < 16, 500000.0 ** (-(e % 8).astype(np.float64) * 2.0 / 16.0), 0.0)
    c[:, C_INVF] = invf.astype(np.float32)
    c[:, C_SGN] = np.where(e < 8, -1.0, np.where(e < 16, 1.0, 0.0))
    c[:, C_EPS] = EPS
    c[:, C_ONE] = 1.0
    c[:, C_ONES:C_ONES + 128] = 1.0
    c[:, C_M0:C_M0 + 64] = 1.0
    c[:, C_M1 + 64:C_M1 + 128] = 1.0
    return c


A_CONST = 0
A_CV = NCONST
A_MISC = NCONST + 512
A_H = 2048
A_X = A_H + 16384
A_Y = A_X + 16384
ARENA = A_Y + 12288

IN_SPECS = [
    ("x", [L, D], F32), ("positions", [L], I32), ("norm_mix", [2, D], F32), ("norm_ffn", [2, D], F32),
    ("norm_final", [D], F32), ("mix_w_in", [1, D, 2560], F32), ("mix_w_out", [1, D, D], F32),
    ("s5_A_re", [1, 32, 64], F32), ("s5_A_im", [1, 32, 64], F32), ("s5_log_dt", [1, 32], F32),
    ("s5_B_re", [1, 32, 64, 16], F32), ("s5_B_im", [1, 32, 64, 16], F32), ("s5_C_re", [1, 32, 16, 64], F32),
    ("s5_C_im", [1, 32, 16, 64], F32), ("s5_D", [1, 32, 16], F32), ("s5_glu_w", [1, 512, 512], F32),
    ("s5_glu_b", [1, 512], F32), ("hgrn_gamma", [2, 512], F32), ("hgrn_norm", [1, 512], F32),
    ("att_w_qkv", [1, D, 4608], F32), ("att_w_o", [1, 512, D], F32), ("ffn_w_in", [2, D, 2 * DFF], F32),
    ("ffn_conv_w", [2, 3, 2 * DFF], F32), ("ffn_conv_b", [2, 2 * DFF], F32), ("ffn_w_out", [2, DFF, D], F32),
    ("consts", [128, NCONST], F32),
]


class KB:
    def __init__(self, mode="full", dbg=False):
        self.mode = mode
        self.nc = nc = bass.Bass("TRN2", target_bir_lowering=False)
        self.P = Prog()
        self.din = {}
        for n, shp, dt in IN_SPECS:
            self.din[n] = nc.dram_tensor(n, shp, dt, kind="ExternalInput")
        self.dout = nc.dram_tensor("out", [L, D], F32, kind="ExternalOutput")
        sk = "ExternalOutput" if dbg else "Internal"
        self.scr = {}
        for n, shp in [("QK", [24, 128, L]), ("VS", [3, 16, 128, 512]), ("PJ", [20, 128, L]), ("VT", [L, 512]),
                       ("US", [32, 128, 256]), ("YS", [32, 128, 256])]:
            self.scr[n] = nc.dram_tensor("scr_" + n, shp, F32, kind=sk)
        self.dbg = dbg

    def sb(self, off, n, p0=0, np_=128):
        return V(self.ar, "sb", ARENA, off, [(1, n)], p0, np_)

    def sbd(self, off, dims, p0=0, np_=128):
        return V(self.ar, "sb", ARENA, off, dims, p0, np_)

    def bf(self, off32, off, n, p0=0, np_=128):
        return V(self.arb, "sb", 2 * ARENA, 2 * off32 + off, [(1, n)], p0, np_, kd=2)

    def psv(self, bank, n=512, off=0, p0=0, np_=128):
        return V(self.ps, "ps", 4096, bank * 512 + off, [(1, n)], p0, np_)

    def psd(self, bank, dims, off=0, p0=0, np_=128):
        return V(self.ps, "ps", 4096, bank * 512 + off, dims, p0, np_)

    def dr(self, name, off, dims, keys):
        t = self.din[name] if name in self.din else (self.scr[name] if name in self.scr else self.dout)
        return DR(bass.AP(t, off, [list(d) for d in dims]), keys)

    def mm(self, out, lhsT, rhs, start=True, stop=True):
        self.P.op("pe", lambda e: e.matmul(out.ap(), lhsT.ap(), rhs.ap(), start=start, stop=stop),
                  reads=_ks(lhsT, rhs), writes=_ks(out))

    def tr(self, out, in_):
        idv = self.sb(A_CONST + C_IDENT, in_.np, 0, in_.np)
        self.P.op("pe", lambda e: e.transpose(out.ap(), in_.ap(), idv.ap()), reads=_ks(in_, idv), writes=_ks(out))

    def act(self, out, in_, func, scale=1.0, bias=0.0, accum=None, eng="act"):
        kw = {}
        if accum is not None:
            kw["accum_out"] = accum.ap()
        self.P.op(eng, lambda e: e.activation(out=out.ap(), in_=in_.ap(), func=func, scale=_a(scale), bias=_a(bias), **kw),
                  reads=_ks(in_, scale, bias), writes=_ks(out, accum))

    def ts(self, eng, out, in0, s1, op0, s2=None, op1=None):
        if op1 is None:
            self.P.op(eng, lambda e: e.tensor_scalar(out=out.ap(), in0=in0.ap(), scalar1=_a(s1), scalar2=None, op0=op0),
                      reads=_ks(in0, s1), writes=_ks(out))
        else:
            self.P.op(eng, lambda e: e.tensor_scalar(out=out.ap(), in0=in0.ap(), scalar1=_a(s1), scalar2=_a(s2), op0=op0, op1=op1),
                      reads=_ks(in0, s1, s2), writes=_ks(out))

    def tt(self, eng, out, in0, in1, op):
        self.P.op(eng, lambda e: e.tensor_tensor(out=out.ap(), in0=in0.ap(), in1=in1.ap(), op=op),
                  reads=_ks(in0, in1), writes=_ks(out))

    def stt(self, out, in0, scalar, in1, op0, op1):
        self.P.op("dve", lambda e: e.scalar_tensor_tensor(out=out.ap(), in0=in0.ap(), scalar=_a(scalar), in1=in1.ap(), op0=op0, op1=op1),
                  reads=_ks(in0, scalar, in1), writes=_ks(out))

    def cp(self, eng, out, in_):
        if eng == "act":
            self.P.op("act", lambda e: e.copy(out=out.ap(), in_=in_.ap()), reads=_ks(in_), writes=_ks(out))
        else:
            self.P.op(eng, lambda e: e.tensor_copy(out=out.ap(), in_=in_.ap()), reads=_ks(in_), writes=_ks(out))

    def memset(self, eng, out, val):
        self.P.op(eng, lambda e: e.memset(out.ap(), val), writes=_ks(out))

    def recip(self, out, in_):
        self.P.op("dve", lambda e: e.reciprocal(out=out.ap(), in_=in_.ap()), reads=_ks(in_), writes=_ks(out))

    def dma(self, q, out, in_):
        self.P.op(q, lambda e: e.dma_start(out=out.ap(), in_=in_.ap()), reads=_ks(in_), writes=_ks(out), dma=True)

    def H(self, kt, t0=0, n=L):
        return self.sb(A_H + kt * L + t0, n)

    def cv(self, name, j=0):
        return self.sb(self.cvoff[name] + j, 1)

    def load_consts(self):
        self.dma("sp", self.sb(A_CONST, NCONST), self.dr("consts", 0, [(NCONST, 128), (1, NCONST)], ["consts"]))
        rows = []
        rows.append(("nm0", "norm_mix", 0, 8)); rows.append(("nm1", "norm_mix", D, 8))
        rows.append(("nf0", "norm_ffn", 0, 8)); rows.append(("nf1", "norm_ffn", D, 8))
        rows.append(("nfin", "norm_final", 0, 8))
        rows.append(("s5D", "s5_D", 0, 4)); rows.append(("glub", "s5_glu_b", 0, 4))
        rows.append(("gam0", "hgrn_gamma", 0, 4)); rows.append(("gam1", "hgrn_gamma", 512, 4))
        rows.append(("hnorm", "hgrn_norm", 0, 4))
        for l in range(2):
            for j in range(3):
                rows.append(("cw%d%d" % (l, j), "ffn_conv_w", (l * 3 + j) * 2 * DFF, 44))
            rows.append(("cb%d" % l, "ffn_conv_b", l * 2 * DFF, 44))
        self.cvoff = {}
        off = A_CV
        groups = []
        cur = []
        n = 0
        for r in rows:
            if n + r[3] > 128:
                groups.append(cur); cur = []; n = 0
            cur.append(r); n += r[3]
        groups.append(cur)
        assert sum(r[3] for r in rows) <= 512
        for gi, grp in enumerate(groups):
            stg = A_Y + (gi % 2) * 128
            p = 0
            for (nm, dn, doff, nr) in grp:
                self.dma("sp", self.sb(stg, 128, p, nr), self.dr(dn, doff, [(128, nr), (1, 128)], [dn]))
                self.cvoff[nm] = off + p
                p += nr
            bank = gi % 2
            self.tr(self.psv(bank, p), self.sb(stg, 128, 0, p))
            self.cp("dve", self.sb(off, p), self.psv(bank, p))
            off += p

    def load_x(self):
        for tt in range(16):
            stg = A_Y + 512 + (tt % 2) * 1024
            self.dma("sp", self.sb(stg, 1024), self.dr("x", tt * 128 * D, [(D, 128), (1, D)], ["x"]))
            for half in range(2):
                bank = (tt * 2 + half) % 4
                for q in range(4):
                    self.tr(self.psv(bank, 128, q * 128), self.sb(stg + (half * 4 + q) * 128, 128))
                eng = "dve" if half == 0 else "act"
                self.cp(eng, self.sbd(A_H + half * 4 * L + tt * 128, [(L, 4), (1, 128)]), self.psd(bank, [(128, 4), (1, 128)]))

    def rmsnorm(self, gname, t0, n, dst_off, dst_stride, tmp_off, bank, bf=False):
        sq = [tmp_off, tmp_off + 512]
        rs = tmp_off + 1024
        acc = self.psv(bank, n)
        for kt in range(8):
            s = self.sb(sq[kt % 2], n)
            self.act(s, self.H(kt, t0, n), AF.Square)
            self.mm(acc, self.sb(A_CONST + C_ONES, 128), s, start=(kt == 0), stop=(kt == 7))
        rsv = self.sb(rs, n)
        self.act(rsv, acc, AF.Sqrt, scale=1.0 / D, bias=self.sb(A_CONST + C_EPS, 1))
        self.recip(rsv, rsv)
        for kt in range(8):
            dv = self.bf(dst_off, kt * dst_stride, n) if bf else self.sb(dst_off + kt * dst_stride, n)
            self.stt(dv, self.H(kt, t0, n), self.cv(gname, kt), rsv, ALU.mult, ALU.mult)

    def load_w(self, dst_off, name, base, row_stride, c0, ncols, nkt=8, q="sp"):
        self.dma(q, self.sbd(dst_off, [(ncols, nkt), (1, ncols)]),
                 self.dr(name, base + c0, [(row_stride, 128), (128 * row_stride, nkt), (1, ncols)], [name]))

    def ffn(self, layer):
        nfn = "nf%d" % layer
        XN = A_X
        G = A_X + 2048
        WIB = [A_X + 7680 + i * 512 for i in range(3)]
        WOB = [A_X + 9216 + i * 1408 for i in range(2)]
        WI = [A_Y + i * 1024 for i in range(3)]
        WO = [A_Y + 3072 + i * 2816 for i in range(2)]
        RAWS = [A_Y + 8704, A_Y + 9728]
        CVA = A_Y + 10752
        CVB = CVA + 512
        SA = CVB + 512
        HALO = A_MISC
        wi_base = layer * D * 2 * DFF
        wo_base = layer * DFF * D
        wcnt = 0
        for b in range(4):
            t0 = b * 512
            self.rmsnorm(nfn, t0, 512, XN, 512, CVA, 7, bf=True)
            for ft in range(22):
                cvs = []
                for ab in range(2):
                    tile = ft + ab * 22
                    wb = WI[wcnt % 3]; wbb = WIB[wcnt % 3]; wcnt += 1
                    self.load_w(wb, "ffn_w_in", wi_base, 2 * DFF, tile * 128, 128)
                    self.cp("act" if ab == 0 else "pool", self.bf(wbb, 0, 1024), self.sb(wb, 1024))
                    bank = (ft * 2 + ab) % 4
                    acc = self.psv(bank)
                    for kt in range(8):
                        self.mm(acc, self.bf(wbb, kt * 128, 128), self.bf(XN, kt * 512, 512), start=(kt == 0), stop=(kt == 7))
                    raw = RAWS[ab]
                    if b == 0:
                        self.memset("pool", self.sb(raw, 2), 0.0)
                    else:
                        self.cp("pool", self.sb(raw, 2), self.sb(HALO + tile * 2, 2))
                    self.cp("act", self.sb(raw + 2, 512), acc)
                    cvt = self.sb(CVA if ab == 0 else CVB, 512)
                    self.act(cvt, acc, AF.Identity, scale=self.cv("cw%d2" % layer, tile), bias=self.cv("cb%d" % layer, tile))
                    self.stt(cvt, self.sb(raw + 1, 512), self.cv("cw%d1" % layer, tile), cvt, ALU.mult, ALU.add)
                    self.stt(cvt, self.sb(raw, 512), self.cv("cw%d0" % layer, tile), cvt, ALU.mult, ALU.add)
                    if b < 3:
                        self.cp("pool", self.sb(HALO + tile * 2, 2), self.sb(raw + 512, 2))
                    cvs.append(cvt)
                sa = self.sb(SA, 512)
                self.act(sa, cvs[0], AF.Silu)
                self.tt("pool", self.bf(G, ft * 512, 512), sa, cvs[1], ALU.mult)
            for dt in range(8):
                wb = WO[dt % 2]; wbb = WOB[dt % 2]
                self.load_w(wb, "ffn_w_out", wo_base, D, dt * 128, 128, nkt=22)
                self.cp("pool" if dt % 2 else "act", self.bf(wbb, 0, 2816), self.sb(wb, 2816))
                acc = self.psv(4 + dt % 2)
                for ft in range(22):
                    self.mm(acc, self.bf(wbb, ft * 128, 128), self.bf(G, ft * 512, 512), start=(ft == 0), stop=(ft == 21))
                hv = self.H(dt, t0, 512)
                self.tt("dve", hv, acc, hv, ALU.add)

    def layer0_mixer(self):
        Y = A_Y
        XN = A_X
        for b in range(4):
            self.rmsnorm("nm0", b * 512, 512, XN + b * 512, L, Y + 8192, 7)
        LB = A_MISC + 128
        OML = A_MISC + 132
        self.tt("dve", self.sb(LB, 4), self.sb(self.cvoff["gam0"], 4), self.sb(self.cvoff["gam1"], 4), ALU.subtract)
        self.act(self.sb(LB, 4), self.sb(LB, 4), AF.Sigmoid)
        self.ts("dve", self.sb(OML, 4), self.sb(LB, 4), -1.0, ALU.mult, 1.0, ALU.add)
        WA = [Y, Y + 1024]
        TO = [Y + 2048, Y + 2560]
        cnt = 0
        for tile in list(range(0, 12)) + list(range(16, 20)):
            wa = WA[cnt % 2]
            self.load_w(wa, "mix_w_in", 0, 2560, tile * 128, 128)
            for b in range(4):
                acc = self.psv(cnt % 2 * 2 + b % 2)
                for kt in range(8):
                    self.mm(acc, self.sb(wa + kt * 128, 128), self.sb(XN + kt * L + b * 512, 512), start=(kt == 0), stop=(kt == 7))
                to = self.sb(TO[b % 2], 512)
                if tile < 4:
                    self.cp("dve", to, acc)
                elif tile < 8 or tile >= 16:
                    self.act(to, acc, AF.Silu)
                else:
                    self.act(to, acc, AF.Sigmoid)
                    self.ts("dve", to, to, self.sb(OML + tile - 8, 1), ALU.mult, self.sb(LB + tile - 8, 1), ALU.add)
                self.dma("pool", self.dr("PJ", tile * 128 * L + b * 512, [(L, 128), (1, 512)], [("PJ", tile, b)]), to)
            cnt += 1
        WV = Y + 4096
        self.load_w(WV, "mix_w_in", 0, 2560, 1536, 512)
        for tt_ in range(16):
            pv = self.psv(4 + tt_ % 2)
            for kt in range(8):
                self.mm(pv, self.sb(XN + kt * L + tt_ * 128, 128), self.sb(WV + kt * 512, 512), start=(kt == 0), stop=(kt == 7))
            vo = self.sb(TO[tt_ % 2], 512)
            self.cp("act" if tt_ % 2 else "dve", vo, pv)
            self.dma("pool", self.dr("VT", tt_ * 128 * 512, [(512, 128), (1, 512)], [("VT", tt_)]), vo)
        if self.mode != "l0b":
            self.hgrn2()
        if self.mode != "l0a":
            self.s5()
        WO = [Y, Y + 1024]
        for dt in range(8):
            wo = WO[dt % 2]
            self.load_w(wo, "mix_w_out", 0, D, dt * 128, 128)
            for b in range(4):
                acc = self.psv((dt * 4 + b) % 4)
                for kt in range(8):
                    self.mm(acc, self.sb(wo + kt * 128, 128), self.sb(A_X + kt * L + b * 512, 512), start=(kt == 0), stop=(kt == 7))
                hv = self.H(dt, b * 512, 512)
                self.tt("dve", hv, acc, hv, ALU.add)
        if self.dbg:
            for kt in range(8):
                self.dma("pool", self.dr("QK", kt * 128 * L, [(L, 128), (1, L)], [("QKdbg", kt)]), self.sb(A_X + kt * L, L))

    def pj_load(self, dst, tile):
        self.dma("sp", self.sb(dst, L), self.dr("PJ", tile * 128 * L, [(L, 128), (1, L)], [("PJ", tile, b) for b in range(4)]))

    def hgrn2(self):
        Y = A_Y
        RM = A_X
        QT = A_X + 2048
        KT = A_X + 4096
        BB = A_X + 6144
        ENB = Y
        KTOK = Y + 2048
        VTOK = Y + 4096
        OH = Y + 6144
        SG = Y + 8192
        SS = [Y + 10240, Y + 10368]
        ATT = Y + 10496
        TMP = Y + 10752
        self.memset("pool", self.sb(RM, L), 1.0)
        self.memset("pool", self.sbd(RM, [(64, 32)]), 0.0)
        for h in range(4):
            self.pj_load(QT, 4 + h)
            self.pj_load(BB, 8 + h)
            self.pj_load(SG, 16 + h)
            self.dma("sp", self.sbd(VTOK, [(128, 16), (1, 128)]),
                     self.dr("VT", h * 128, [(512, 128), (128 * 512, 16), (1, 128)], [("VT", t) for t in range(16)]))
            f = self.sb(BB, L)
            kk = self.sb(KT, L)
            self.ts("dve", kk, f, -1.0, ALU.mult, 1.0, ALU.add)
            self.act(f, f, AF.Ln)
            self.P.op("dve", lambda e, o=self.sb(BB, L), d0=self.sb(RM, L): e.tensor_tensor_scan(
                out=o.ap(), data0=d0.ap(), data1=o.ap(), initial=0.0, op0=ALU.mult, op1=ALU.add),
                reads=_ks(self.sb(BB, L), self.sb(RM, L)), writes=_ks(self.sb(BB, L)))
            self.act(self.sb(ENB, L), f, AF.Exp, scale=-1.0)
            self.act(f, f, AF.Exp)
            self.tt("dve", self.sb(QT, L), self.sb(QT, L), f, ALU.mult)
            self.tt("pool", kk, kk, self.sb(ENB, L), ALU.mult)
            for t_ in range(16):
                pb = self.psv(t_ % 2, 128)
                self.tr(pb, self.sb(KT + t_ * 128, 128))
                self.cp("act" if t_ % 2 else "dve", self.sb(KTOK + t_ * 128, 128), pb)
            for c in range(32):
                t_, half = c // 2, c % 2
                p0 = 64 * half
                c0 = c * 64
                pa = self.psv(2 + c % 2, 64, 0, p0, 64)
                self.mm(pa, self.sb(KT + c0, 64), self.sb(QT + c0, 64))
                att = self.sb(ATT + (c % 2) * 64, 64, p0, 64)
                self.tt("dve", att, pa, self.sb(A_CONST + C_HGM, 64, p0, 64), ALU.mult)
                po = self.psv(4 + c % 2, 64)
                vt = self.sb(VTOK + t_ * 128, 128, p0, 64)
                S_old = self.sb(SS[(c + 1) % 2], 128)
                S_new = self.sb(SS[c % 2], 128)
                self.mm(po, vt, att, start=True, stop=(c == 0))
                if c > 0:
                    self.mm(po, S_old, self.sb(QT + c0, 64), start=False, stop=True)
                self.cp("act", self.sb(OH + c0, 64), po)
                if c < 31:
                    pS = self.psv(6 + c % 2, 128)
                    self.mm(pS, self.sb(KTOK + t_ * 128, 128, p0, 64), vt, start=True, stop=(c == 0))
                    if c > 0:
                        self.mm(pS, self.sb(A_CONST + C_IDENT, 128), S_old, start=False, stop=True)
                    self.ts("dve", S_new, pS, self.sb(BB + c0 + 63, 1), ALU.mult)
            for b in range(4):
                sq = self.sb(TMP, 512)
                self.act(sq, self.sb(OH + b * 512, 512), AF.Square)
                pn = self.psv(b % 2)
                self.mm(pn, self.sb(A_CONST + C_ONES, 128), sq)
                rs = self.sb(TMP + 512, 512)
                self.act(rs, pn, AF.Sqrt, scale=1.0 / 128, bias=self.sb(A_CONST + C_EPS, 1))
                self.recip(rs, rs)
                ob = self.sb(A_X + (4 + h) * L + b * 512, 512)
                self.stt(ob, self.sb(OH + b * 512, 512), self.cv("hnorm", h), rs, ALU.mult, ALU.mult)
                self.tt("pool", ob, ob, self.sb(SG + b * 512, 512), ALU.mult)

    def s5(self):
        Y = A_Y
        UT = Y
        PP = [(Y + 2048, Y + 4096), (Y + 6144, Y + 8192)]
        SM = Y + 10240
        EB = [SM, SM + 128]
        BD = [SM + 256, SM + 384]
        EC = [SM + 512, SM + 640]
        BT = SM + 768
        SC = SM + 896
        GT = SM + 1152
        it16 = V(self.it, "it", 2048, 0, [(1, 16)])

        def g(i):
            return self.sb(GT + 16 * i, 16)
        ARE, AIM, DT, AR, AI, CRE, CIM, T0, T1, T2 = [g(i) for i in range(10)]
        PWB = GT + 160

        def pw(k, j):
            return self.sb(PWB + (k * 3 + j) * 16, 16)
        for name, dst in (("s5_A_re", ARE), ("s5_A_im", AIM)):
            st = self.sb(SC, 128, 0, 16)
            self.dma("sp", st, self.dr(name, 0, [(128, 16), (1, 128)], [name]))
            pb = self.psv(0, 16)
            self.tr(pb, st)
            self.cp("dve", dst, pb)
        ld = self.sb(BT, 32)
        self.dma("sp", ld, self.dr("s5_log_dt", 0, [(0, 128), (1, 32)], ["s5_log_dt"]))
        self.cp("dve", self.sb(GT + 32, 16, 0, 64), self.sbd(BT, [(2, 16)], 0, 64))
        self.cp("dve", self.sb(GT + 32, 16, 64, 64), self.sbd(BT + 1, [(2, 16)], 64, 64))
        pb = DT
        self.act(DT, pb, AF.Exp)
        self.tt("dve", T0, ARE, DT, ALU.mult)
        self.act(T0, T0, AF.Exp)
        self.tt("dve", T1, AIM, DT, ALU.mult)
        self.ts("dve", T1, T1, 1.0 / (2 * np.pi), ALU.mult)
        for dst, shift in ((AI, 0.0), (AR, 0.25)):
            if shift:
                self.ts("dve", T1, T1, shift, ALU.add)
            self.cp("dve", it16, T1)
            self.cp("dve", T2, it16)
            self.tt("dve", T2, T1, T2, ALU.subtract)
            self.act(dst, T2, AF.Sin, scale=float(2 * np.pi))
        self.tt("dve", AR, AR, T0, ALU.mult)
        self.tt("dve", AI, AI, T0, ALU.mult)
        self.tt("dve", T0, ARE, ARE, ALU.mult)
        self.tt("dve", T1, AIM, AIM, ALU.mult)
        self.tt("dve", T0, T0, T1, ALU.add)
        self.recip(T0, T0)
        self.ts("dve", T1, AR, -1.0, ALU.add)
        self.tt("dve", CRE, T1, ARE, ALU.mult)
        self.tt("dve", T2, AI, AIM, ALU.mult)
        self.tt("dve", CRE, CRE, T2, ALU.add)
        self.tt("dve", CRE, CRE, T0, ALU.mult)
        self.tt("dve", CIM, AI, ARE, ALU.mult)
        self.tt("dve", T2, T1, AIM, ALU.mult)
        self.tt("dve", CIM, CIM, T2, ALU.subtract)
        self.tt("dve", CIM, CIM, T0, ALU.mult)
        self.cp("dve", pw(0, 0), AR)
        self.cp("dve", pw(0, 1), AI)
        for k in range(11):
            if k > 0:
                self.tt("dve", T0, pw(k - 1, 0), pw(k - 1, 0), ALU.mult)
                self.tt("dve", T1, pw(k - 1, 1), pw(k - 1, 1), ALU.mult)
                self.tt("dve", pw(k, 0), T0, T1, ALU.subtract)
                self.tt("dve", T0, pw(k - 1, 0), pw(k - 1, 1), ALU.mult)
                self.ts("dve", pw(k, 1), T0, 2.0, ALU.mult)
            self.ts("dve", pw(k, 2), pw(k, 1), -1.0, ALU.mult)
        for q in range(16):
            ct, gq = q // 4, q % 4
            if gq == 0:
                self.pj_load(UT, ct)
                for b in range(4):
                    self.ts("pool", self.sb(A_X + ct * L + b * 512, 512), self.sb(UT + b * 512, 512), self.cv("s5D", ct), ALU.mult)
            btr, bti, bbr, bbi, btmp = [self.sb(BT + 16 * i, 16) for i in range(5)]
            self.dma("sp", btr, self.dr("s5_B_re", q * 2048, [(16, 128), (1, 16)], ["s5_B_re"]))
            self.dma("sp", bti, self.dr("s5_B_im", q * 2048, [(16, 128), (1, 16)], ["s5_B_im"]))
            cre, cim = self.sb(GT + 16 * 5 + q, 1), self.sb(GT + 16 * 6 + q, 1)
            self.ts("dve", btmp, bti, cim, ALU.mult)
            self.stt(bbr, btr, cre, btmp, ALU.mult, ALU.subtract)
            self.ts("dve", btmp, btr, cim, ALU.mult)
            self.stt(bbi, bti, cre, btmp, ALU.mult, ALU.add)
            for part, bb in ((0, bbr), (1, bbi)):
                self.memset("pool", self.sb(EB[part], 128), 0.0)
                for g2 in range(2):
                    self.cp("pool", self.sb(EB[part] + 32 * gq + 16 * g2, 16, 64 * g2, 64), self.sb(BT + 16 * (2 + part), 16, 64 * g2, 64))
                pb = self.psv(2 + part, 128)
                self.tr(pb, self.sb(EB[part], 128))
                self.cp("act", self.sb(BD[part], 128), pb)
            for part, name in ((0, "s5_C_re"), (1, "s5_C_im")):
                st = self.sb(SC + 128 * part, 128, 0, 16)
                self.dma("sp", self.sbd(SC + 128 * part, [(64, 2), (1, 64)], 0, 16), self.dr(name, q * 2048, [(64, 16), (1024, 2), (1, 64)], [name]))
                pb = self.psv(4 + part, 16)
                self.tr(pb, st)
                self.memset("pool", self.sb(EC[part], 128), 0.0)
                for g2 in range(2):
                    dst = self.sb(EC[part] + 32 * gq + 16 * g2, 16, 64 * g2, 64)
                    src = self.psv(4 + part, 16, 0, 64 * g2, 64)
                    if part == 0:
                        self.cp("dve", dst, src)
                    else:
                        self.ts("dve", dst, src, -1.0, ALU.mult)
            cur = 0
            for part in range(2):
                for b in range(4):
                    pb = self.psv(6 + b % 2)
                    self.mm(pb, self.sb(BD[part], 128), self.sb(UT + b * 512, 512))
                    self.cp("act", self.sb(PP[0][part] + b * 512, 512), pb)
            for k in range(11):
                sft = 1 << k
                n = L - sft
                a_re, a_im = PP[cur]
                b_re, b_im = PP[1 - cur]
                sar = self.sb(PWB + (k * 3 + 0) * 16 + q, 1)
                sai = self.sb(PWB + (k * 3 + 1) * 16 + q, 1)
                snai = self.sb(PWB + (k * 3 + 2) * 16 + q, 1)
                self.stt(self.sb(b_re + sft, n), self.sb(a_re, n), sar, self.sb(a_re + sft, n), ALU.mult, ALU.add)
                self.stt(self.sb(b_re + sft, n), self.sb(a_im, n), snai, self.sb(b_re + sft, n), ALU.mult, ALU.add)
                self.stt(self.sb(b_im + sft, n), self.sb(a_im, n), sar, self.sb(a_im + sft, n), ALU.mult, ALU.add)
                self.stt(self.sb(b_im + sft, n), self.sb(a_re, n), sai, self.sb(b_im + sft, n), ALU.mult, ALU.add)
                self.cp("pool", self.sb(b_re, sft), self.sb(a_re, sft))
                self.cp("pool", self.sb(b_im, sft), self.sb(a_im, sft))
                cur = 1 - cur
            x_re, x_im = PP[cur]
            for b in range(4):
                pb = self.psv(b % 2)
                self.mm(pb, self.sb(EC[0], 128), self.sb(x_re + b * 512, 512), start=True, stop=False)
                self.mm(pb, self.sb(EC[1], 128), self.sb(x_im + b * 512, 512), start=False, stop=True)
                yv = self.sb(A_X + ct * L + b * 512, 512)
                self.tt("dve", yv, pb, yv, ALU.add)
        TG = Y
        for ct in range(4):
            yv = self.sb(A_X + ct * L, L)
            t = self.sb(TG + (ct % 2) * 2048, L)
            self.act(t, yv, AF.Square)
            self.ts("dve", t, t, 0.044715, ALU.mult, 1.0, ALU.add)
            self.tt("pool", t, t, yv, ALU.mult)
            self.act(t, t, AF.Sigmoid, scale=1.5957691216057308)
            self.tt("dve", yv, yv, t, ALU.mult)
        GW = SM
        GG = Y
        for ot in range(4):
            self.dma("sp", self.sbd(GW, [(128, 4), (1, 128)]), self.dr("s5_glu_w", ot * 128, [(512, 128), (128 * 512, 4), (1, 128)], ["s5_glu_w"]))
            for b in range(4):
                pb = self.psv(2 + b % 2)
                for kt in range(4):
                    self.mm(pb, self.sb(GW + kt * 128, 128), self.sb(A_X + kt * L + b * 512, 512), start=(kt == 0), stop=(kt == 3))
                self.act(self.sb(GG + ot * L + b * 512, 512), pb, AF.Sigmoid, bias=self.cv("glub", ot))
        for ct in range(4):
            yv = self.sb(A_X + ct * L, L)
            self.tt("dve" if ct % 2 else "pool", yv, yv, self.sb(GG + ct * L, L), ALU.mult)

    def rot_tables(self, COS, SIN, R, NF):
        it = V(self.it, "it", 2048, 0, [(1, L)])
        self.dma("sp", it, self.dr("positions", 0, [(0, 128), (1, L)], ["positions"]))
        r = self.sb(R, L)
        nf = self.sb(NF, L)
        self.cp("dve", r, it)
        self.ts("dve", r, r, self.sb(A_CONST + C_INVF, 1), ALU.mult, 1.0 / (2 * np.pi), ALU.mult)
        for dst, shift in ((SIN, 0.0), (COS, 0.25)):
            if shift:
                self.ts("dve", r, r, shift, ALU.add)
            self.cp("dve", it, r)
            self.cp("dve", nf, it)
            self.tt("dve", nf, r, nf, ALU.subtract)
            self.act(self.sb(dst, L), nf, AF.Sin, scale=float(2 * np.pi))
        self.ts("dve", self.sb(SIN, L), self.sb(SIN, L), self.sb(A_CONST + C_SGN, 1), ALU.mult)

    def layer1_mixer(self):
        Y = A_Y
        COS, SIN = Y, Y + 2048
        WA = [Y + 4096, Y + 5120]
        WS = [Y + 6144, Y + 7168]
        T1 = [Y + 8192, Y + 8704]
        T2 = [Y + 9216, Y + 9728]
        QO = [Y + 10240, Y + 10752]
        XN = A_X
        for b in range(4):
            self.rmsnorm("nm1", b * 512, 512, XN + b * 512, L, T1[0], 7)
        self.rot_tables(COS, SIN, WA[0], WS[0])
        for w in WS:
            self.memset("pool", self.sb(w, 1024), 0.0)
        cnt = 0
        for i in range(24):
            wa, ws = WA[i % 2], WS[i % 2]
            self.load_w(wa, "att_w_qkv", 0, 4608, i * 128, 128)
            for lo, hi in ((0, 8), (8, 0)):
                self.cp("pool", self.sbd(ws + lo, [(128, 8), (64, 2), (1, 8)]), self.sbd(wa + hi, [(128, 8), (64, 2), (1, 8)]))
            for b in range(4):
                pa = self.psv(cnt % 2)
                pb = self.psv(2 + cnt % 2)
                for kt in range(8):
                    self.mm(pa, self.sb(wa + kt * 128, 128), self.sb(XN + kt * L + b * 512, 512), start=(kt == 0), stop=(kt == 7))
                for kt in range(8):
                    self.mm(pb, self.sb(ws + kt * 128, 128), self.sb(XN + kt * L + b * 512, 512), start=(kt == 0), stop=(kt == 7))
                t1 = self.sb(T1[cnt % 2], 512)
                t2 = self.sb(T2[cnt % 2], 512)
                qo = self.sb(QO[cnt % 2], 512)
                self.tt("dve", t1, pa, self.sb(COS + b * 512, 512), ALU.mult)
                self.tt("dve", t2, pb, self.sb(SIN + b * 512, 512), ALU.mult)
                self.tt("pool", qo, t1, t2, ALU.add)
                self.dma("pool", self.dr("QK", i * 128 * L + b * 512, [(L, 128), (1, 512)], [("QK", i, b)]), qo)
                cnt += 1
        WV = Y + 4096
        for g in range(3):
            dil = (1, 4, 16)[g]
            nbk = 16 // dil
            self.load_w(WV, "att_w_qkv", 0, 4608, 3072 + g * 512, 512)
            for blk in range(16):
                r, n = blk // nbk, blk % nbk
                base = r + dil * 128 * n
                pv = self.psv(4 + blk % 2)
                for kt in range(8):
                    self.mm(pv, self.sbd(XN + kt * L + base, [(dil, 128)]), self.sb(WV + kt * 512, 512), start=(kt == 0), stop=(kt == 7))
                vo = self.sb(T1[0] + (blk % 2) * 1024, 512)
                self.cp("act" if blk % 2 else "dve", vo, pv)
                self.dma("pool", self.dr("VS", (g * 16 + blk) * 128 * 512, [(512, 128), (1, 512)], [("VS", g, blk)]), vo)
        OT = Y
        PT = [Y + 8192, Y + 8704, Y + 9216]
        RD = Y + 9728
        WO = Y + 11776
        ACC_O = A_X + 12288
        ACC_D = A_X + 14336
        ones64 = self.sb(A_CONST + C_ONES, 64)
        ucnt = 0
        pcnt = 0
        ocnt = 0
        for j in range(4):
            for g in range(3):
                dil = (1, 4, 16)[g]
                nbk = 16 // dil
                base_x = A_X + (ucnt % 2) * 6144
                ucnt += 1
                QT, KT, VB = base_x, base_x + 2048, base_x + 4096
                self.dma("sp", self.sb(QT, L), self.dr("QK", (g * 4 + j) * 128 * L, [(L, 128), (1, L)], [("QK", g * 4 + j, b) for b in range(4)]))
                self.dma("sp", self.sb(KT, L), self.dr("QK", (12 + g * 4 + j) * 128 * L, [(L, 128), (1, L)], [("QK", 12 + g * 4 + j, b) for b in range(4)]))
                self.dma("sp", self.sbd(VB, [(128, 16), (1, 128)]),
                         self.dr("VS", g * 16 * 128 * 512 + j * 128, [(512, 128), (128 * 512, 16), (1, 128)], [("VS", g, blk) for blk in range(16)]))
                for hh in range(2):
                    p0 = 64 * hh
                    for r in range(dil):
                        prev = None
                        for n in range(nbk):
                            blk = r * nbk + n
                            base = r + dil * 128 * n
                            nq = 256 if n + 1 < nbk else 128
                            st = self.psv(pcnt % 3, nq)
                            self.mm(st, self.sbd(KT + base, [(dil, 128)], p0, 64), self.sbd(QT + base, [(dil, nq)], p0, 64))
                            pt = self.sb(PT[pcnt % 3], nq)
                            self.act(pt, st, AF.Exp, scale=0.125)
                            self.tt("pool" if pcnt % 2 else "dve", pt, pt, self.sb(A_CONST + C_ATTM, nq), ALU.mult)
                            po = self.psv(4 + ocnt % 2, 128, 0, p0, 64)
                            pd = self.psv(6 + ocnt % 2, 128, 0, p0, 64)
                            ocnt += 1
                            vcur = self.sb(VB + blk * 128 + hh * 64, 64)
                            if prev is not None:
                                vprev = self.sb(VB + (blk - 1) * 128 + hh * 64, 64)
                                pprev = self.sb(prev + 128, 128)
                                self.mm(po, vprev, pprev, start=True, stop=False)
                                self.mm(po, vcur, self.sb(PT[pcnt % 3], 128), start=False, stop=True)
                                self.mm(pd, ones64, pprev, start=True, stop=False)
                                self.mm(pd, ones64, self.sb(PT[pcnt % 3], 128), start=False, stop=True)
                            else:
                                self.mm(po, vcur, self.sb(PT[pcnt % 3], 128), start=True, stop=True)
                                self.mm(pd, ones64, self.sb(PT[pcnt % 3], 128), start=True, stop=True)
                            ao = self.sbd(ACC_O + base, [(dil, 128)], p0, 64)
                            ad = self.sbd(ACC_D + base, [(dil, 128)], p0, 64)
                            if g == 0:
                                self.cp("act", ao, po)
                                self.cp("act", ad, pd)
                            else:
                                self.tt("dve", ao, po, ao, ALU.add)
                                self.tt("dve", ad, pd, ad, ALU.add)
                            prev = PT[pcnt % 3]
                            pcnt += 1
            rd = self.sb(RD, L)
            self.recip(rd, self.sb(ACC_D, L))
            self.tt("dve", self.sb(OT + j * L, L), self.sb(ACC_O, L), rd, ALU.mult)
        for dt in range(8):
            self.dma("sp", self.sbd(WO, [(128, 4), (1, 128)]), self.dr("att_w_o", dt * 128, [(D, 128), (128 * D, 4), (1, 128)], ["att_w_o"]))
            for b in range(4):
                acc = self.psv((dt * 4 + b) % 4)
                for j in range(4):
                    self.mm(acc, self.sb(WO + j * 128, 128), self.sb(OT + j * L + b * 512, 512), start=(j == 0), stop=(j == 3))
                hv = self.H(dt, b * 512, 512)
                self.tt("dve", hv, acc, hv, ALU.add)

    def final(self):
        XN = A_X
        for b in range(4):
            t0 = b * 512
            self.rmsnorm("nfin", t0, 512, XN, 512, A_Y, 7)
            for ts_ in range(4):
                stg = A_Y + 2048 + (ts_ % 2) * 1024
                for half in range(2):
                    bank = (ts_ * 2 + half) % 4
                    for q in range(4):
                        kt = half * 4 + q
                        self.tr(self.psv(bank, 128, q * 128), self.sb(XN + kt * 512 + ts_ * 128, 128))
                    self.cp("dve" if half == 0 else "act", self.sb(stg + half * 512, 512), self.psv(bank))
                tt = b * 4 + ts_
                self.dma("pool", self.dr("out", tt * 128 * D, [(D, 128), (1, D)], ["out%d" % tt]), self.sb(stg, 1024))

    def build(self):
        nc = self.nc
        with ExitStack() as es:
            self.ar = es.enter_context(nc.sbuf_tensor("arena", [128, ARENA], F32))
            self.ps = es.enter_context(nc.psum_tensor("psum", [128, 4096], F32))
            self.arb = self.ar.bitcast(BF16)
            es.enter_context(nc.allow_low_precision("bf16 matmul operands (fp32 PSUM accumulation) in the FFN"))
            self.it = es.enter_context(nc.sbuf_tensor("itile", [128, 2048], I32))
            self.load_consts()
            self.load_x()
            m = self.mode
            if m in ("full", "l0", "l0a", "l0b"):
                self.layer0_mixer()
            if m in ("full", "l0", "ffn0") and m not in ("l0a", "l0b"):
                self.ffn(0)
            if m in ("full", "l1", "l1y"):
                self.layer1_mixer()
            if m == "l1y":
                self.memset("pool", self.sb(A_Y, 16), 0.0)
            if m in ("full", "l1", "ffn1", "l1y"):
                self.ffn(1)
            self.final()
            P = self.P
            P.emit(lambda name: es.enter_context(nc.semaphore(name)))
            with nc.Block() as block:
                @block.sync
                def _(e):
                    P.run_engine("sp", e)

                @block.tensor
                def _(e):
                    P.run_engine("pe", e)

                @block.scalar
                def _(e):
                    P.run_engine("act", e)

                @block.vector
                def _(e):
                    P.run_engine("dve", e)

                @block.gpsimd
                def _(e):
                    P.run_engine("pool", e)
        return nc


def kernel(**inputs):
    mode = inputs.pop("_mode", "full")
    dbg = inputs.pop("_dbg", False)
    ncores = inputs.pop("_ncores", 8)
    kb = KB(mode, dbg)
    nc = kb.build()
    consts = _const_table()
    in_maps = []
    for c in range(ncores):
        m = {}
        for n, shp, dt in IN_SPECS:
            if n == "consts":
                m[n] = consts
            elif n == "x":
                m[n] = np.ascontiguousarray(inputs["x"][c], dtype=np.float32)
            elif n == "positions":
                m[n] = np.ascontiguousarray(inputs["positions"][c], dtype=np.int32)
            else:
                m[n] = np.ascontiguousarray(inputs[n])
        in_maps.append(m)
    if inputs.get("_trace"):
        res = run_bass_kernel_spmd(nc, in_maps, core_ids=list(range(ncores)), trace=True)
        print("EXEC_TIME_NS", res.exec_time_ns)
    else:
        res = run_bass_kernel_spmd(nc, in_maps, core_ids=list(range(ncores)))
    if dbg:
        return res.results
    return np.stack([r["out"] for r in res.results], axis=0).astype(np.float32)
```

```python
import numpy as np
from contextlib import ExitStack
import concourse.bass as bass
import concourse.mybir as mybir
from concourse.bass_utils import run_bass_kernel_spmd

F32 = mybir.dt.float32
BF16 = mybir.dt.bfloat16
I32 = mybir.dt.int32
AF = mybir.ActivationFunctionType
ALU = mybir.AluOpType

ENGS = ("pe", "act", "dve", "pool", "sp")
NDMA_SEM = 8
L = 2048
D = 1024
DFF = 2816
EPS = 1e-6


class _Op:
    __slots__ = ("eng", "fn", "deps", "dma", "sig", "idx", "name", "dj")


class Prog:
    def __init__(self):
        self.ops = []
        self.last_writer = {}
        self.readers = {}

    def op(self, eng, fn, reads=(), writes=(), dma=False, name=""):
        deps = set()
        lw = self.last_writer
        rd = self.readers
        for k in reads:
            w = lw.get(k)
            if w is not None:
                deps.add(w)
        for k in writes:
            w = lw.get(k)
            if w is not None:
                deps.add(w)
            r = rd.get(k)
            if r:
                deps.update(r)
        o = _Op()
        o.eng, o.fn, o.dma, o.name = eng, fn, dma, name
        o.idx = len(self.ops)
        deps.discard(o.idx)
        o.deps = deps
        o.sig = None
        self.ops.append(o)
        for k in reads:
            rd.setdefault(k, []).append(o.idx)
        for k in writes:
            lw[k] = o.idx
            rd[k] = []
        return o.idx

    def _skip(self, od, o):
        return od.eng == "pe" and o.eng == "pe" and not od.dma and not o.dma

    def emit(self, sem_ctx):
        ops = self.ops
        needed = [False] * len(ops)
        for o in ops:
            for d in o.deps:
                if not self._skip(ops[d], o):
                    needed[d] = True
        eng_sem = {}
        cnt = {e: 0 for e in ENGS}
        dcnt = {e: 0 for e in ENGS}
        dma_sems = {}
        per_eng = {e: [] for e in ENGS}
        for o in ops:
            per_eng[o.eng].append(o)
            if o.dma:
                j = dcnt[o.eng]
                dcnt[o.eng] += 1
                key = (o.eng, j % NDMA_SEM)
                if key not in dma_sems:
                    dma_sems[key] = sem_ctx("d%s%d" % key)
                o.sig = (dma_sems[key], 16 * (j // NDMA_SEM + 1), 16)
                o.dj = j
            elif needed[o.idx]:
                if o.eng not in eng_sem:
                    eng_sem[o.eng] = sem_ctx("c" + o.eng)
                cnt[o.eng] += 1
                o.sig = (eng_sem[o.eng], cnt[o.eng], 1)
        self.per_eng = per_eng
        self.dma_lists = {e: [o for o in per_eng[e] if o.dma] for e in ENGS}
        self._deadlock_check()

    def _waits_for(self, o):
        ops = self.ops
        w = []
        for d in o.deps:
            od = ops[d]
            if self._skip(od, o):
                continue
            w.append(od.sig[:2])
        if o.dma and o.dj >= NDMA_SEM:
            w.append(self.dma_lists[o.eng][o.dj - NDMA_SEM].sig[:2])
        return w

    def _deadlock_check(self):
        per_eng = self.per_eng
        semval = {}
        pos = {e: 0 for e in ENGS}
        total = sum(len(v) for v in per_eng.values())
        done = 0
        while done < total:
            prog = False
            for e in ENGS:
                lst = per_eng[e]
                while pos[e] < len(lst):
                    o = lst[pos[e]]
                    if not all(semval.get(id(s), 0) >= v for s, v in self._waits_for(o)):
                        break
                    if o.sig is not None:
                        semval[id(o.sig[0])] = semval.get(id(o.sig[0]), 0) + o.sig[2]
                    pos[e] += 1
                    done += 1
                    prog = True
            if not prog:
                raise RuntimeError("deadlock in op graph")

    def run_engine(self, e, engobj):
        lst = self.per_eng[e]
        seen = {}
        for o in lst:
            for s, v in self._waits_for(o):
                if seen.get(id(s), 0) >= v:
                    continue
                seen[id(s)] = v
                engobj.wait_ge(s, v)
            ins = o.fn(engobj)
            if o.sig is not None:
                ins.then_inc(o.sig[0], o.sig[2])
        last = {}
        for o in self.dma_lists[e]:
            last[id(o.sig[0])] = o.sig[:2]
        for s, v in last.values():
            if seen.get(id(s), 0) < v:
                engobj.wait_ge(s, v)


class V:
    __slots__ = ("t", "sp", "off", "dims", "p0", "np", "F", "kd")

    def __init__(self, t, sp, F, off, dims, p0=0, np_=128, kd=1):
        self.t, self.sp, self.F, self.off, self.dims, self.p0, self.np, self.kd = t, sp, F, off, dims, p0, np_, kd

    def ap(self):
        return bass.AP(self.t, self.p0 * self.F + self.off, [[self.F, self.np]] + [[s, n] for s, n in self.dims])

    def keys(self):
        lo = self.off // self.kd
        hi = (self.off + sum((n - 1) * s for s, n in self.dims)) // self.kd
        ks = []
        for q in range(self.p0 // 32, (self.p0 + self.np - 1) // 32 + 1):
            for c in range(lo // 512, hi // 512 + 1):
                ks.append((self.sp, q, c))
        return ks


class DR:
    __slots__ = ("a", "k")

    def __init__(self, a, k):
        self.a, self.k = a, list(k)

    def ap(self):
        return self.a

    def keys(self):
        return self.k


def _ks(*vs):
    out = []
    for v in vs:
        if isinstance(v, (V, DR)):
            out.extend(v.keys())
    return out


def _a(v):
    return v.ap() if isinstance(v, (V, DR)) else v


C_IDENT = 0
C_ATTM = 128
C_HGM = 384
C_S5M = 448
C_INVF = 576
C_SGN = 577
C_EPS = 578
C_ONE = 579
C_ONES = 640
C_M0 = 768
C_M1 = 896
NCONST = 1024


def _const_table():
    c = np.zeros((128, NCONST), np.float32)
    c[:, C_IDENT:C_IDENT + 128] = np.eye(128, dtype=np.float32)
    k = np.arange(128)[:, None]
    q = np.arange(128)[None, :]
    c[:, C_ATTM:C_ATTM + 128] = (q >= k)
    c[:, C_ATTM + 128:C_ATTM + 256] = (k >= q)
    s = (np.arange(128) % 64)[:, None]
    t = np.arange(64)[None, :]
    c[:, C_HGM:C_HGM + 64] = (s <= t)
    j = (np.arange(128) // 16)[:, None]
    i = (np.arange(128) // 16)[None, :]
    c[:, C_S5M:C_S5M + 128] = (i >= j)
    e = np.arange(128) % 64
    invf = np.where(e < 16, 500000.0 ** (-(e % 8).astype(np.float64) * 2.0 / 16.0), 0.0)
    c[:, C_INVF] = invf.astype(np.float32)
    c[:, C_SGN] = np.where(e < 8, -1.0, np.where(e < 16, 1.0, 0.0))
    c[:, C_EPS] = EPS
    c[:, C_ONE] = 1.0
    c[:, C_ONES:C_ONES + 128] = 1.0
    c[:, C_M0:C_M0 + 64] = 1.0
    c[:, C_M1 + 64:C_M1 + 128] = 1.0
    return c


A_CONST = 0
A_CV = NCONST
A_MISC = NCONST + 512
A_H = 2048
A_X = A_H + 16384
A_Y = A_X + 16384
ARENA = A_Y + 12288

IN_SPECS = [
    ("x", [L, D], F32), ("positions", [L], I32), ("norm_mix", [2, D], F32), ("norm_ffn", [2, D], F32),
    ("norm_final", [D], F32), ("mix_w_in", [1, D, 2560], F32), ("mix_w_out", [1, D, D], F32),
    ("s5_A_re", [1, 32, 64], F32), ("s5_A_im", [1, 32, 64], F32), ("s5_log_dt", [1, 32], F32),
    ("s5_B_re", [1, 32, 64, 16], F32), ("s5_B_im", [1, 32, 64, 16], F32), ("s5_C_re", [1, 32, 16, 64], F32),
    ("s5_C_im", [1, 32, 16, 64], F32), ("s5_D", [1, 32, 16], F32), ("s5_glu_w", [1, 512, 512], F32),
    ("s5_glu_b", [1, 512], F32), ("hgrn_gamma", [2, 512], F32), ("hgrn_norm", [1, 512], F32),
    ("att_w_qkv", [1, D, 4608], F32), ("att_w_o", [1, 512, D], F32), ("ffn_w_in", [2, D, 2 * DFF], F32),
    ("ffn_conv_w", [2, 3, 2 * DFF], F32), ("ffn_conv_b", [2, 2 * DFF], F32), ("ffn_w_out", [2, DFF, D], F32),
    ("consts", [128, NCONST], F32),
]


class KB:
    def __init__(self, mode="full", dbg=False):
        self.mode = mode
        self.nc = nc = bass.Bass("TRN2", target_bir_lowering=False)
        self.P = Prog()
        self.din = {}
        for n, shp, dt in IN_SPECS:
            self.din[n] = nc.dram_tensor(n, shp, dt, kind="ExternalInput")
        self.dout = nc.dram_tensor("out", [L, D], F32, kind="ExternalOutput")
        sk = "ExternalOutput" if dbg else "Internal"
        self.scr = {}
        for n, shp in [("QK", [24, 128, L]), ("VS", [3, 16, 128, 512]), ("PJ", [20, 128, L]), ("VT", [L, 512]),
                       ("US", [32, 128, 256]), ("YS", [32, 128, 256])]:
            self.scr[n] = nc.dram_tensor("scr_" + n, shp, F32, kind=sk)
        self.dbg = dbg

    def sb(self, off, n, p0=0, np_=128):
        return V(self.ar, "sb", ARENA, off, [(1, n)], p0, np_)

    def sbd(self, off, dims, p0=0, np_=128):
        return V(self.ar, "sb", ARENA, off, dims, p0, np_)

    def bf(self, off32, off, n, p0=0, np_=128):
        return V(self.arb, "sb", 2 * ARENA, 2 * off32 + off, [(1, n)], p0, np_, kd=2)

    def bfd(self, off32, off, dims, p0=0, np_=128):
        return V(self.arb, "sb", 2 * ARENA, 2 * off32 + off, dims, p0, np_, kd=2)

    def psv(self, bank, n=512, off=0, p0=0, np_=128):
        return V(self.ps, "ps", 4096, bank * 512 + off, [(1, n)], p0, np_)

    def psd(self, bank, dims, off=0, p0=0, np_=128):
        return V(self.ps, "ps", 4096, bank * 512 + off, dims, p0, np_)

    def dr(self, name, off, dims, keys):
        t = self.din[name] if name in self.din else (self.scr[name] if name in self.scr else self.dout)
        return DR(bass.AP(t, off, [list(d) for d in dims]), keys)

    def mm(self, out, lhsT, rhs, start=True, stop=True):
        self.P.op("pe", lambda e: e.matmul(out.ap(), lhsT.ap(), rhs.ap(), start=start, stop=stop),
                  reads=_ks(lhsT, rhs), writes=_ks(out))

    def tr(self, out, in_):
        idv = self.sb(A_CONST + C_IDENT, in_.np, 0, in_.np)
        self.P.op("pe", lambda e: e.transpose(out.ap(), in_.ap(), idv.ap()), reads=_ks(in_, idv), writes=_ks(out))

    def act(self, out, in_, func, scale=1.0, bias=0.0, accum=None, eng="act"):
        kw = {}
        if accum is not None:
            kw["accum_out"] = accum.ap()
        self.P.op(eng, lambda e: e.activation(out=out.ap(), in_=in_.ap(), func=func, scale=_a(scale), bias=_a(bias), **kw),
                  reads=_ks(in_, scale, bias), writes=_ks(out, accum))

    def ts(self, eng, out, in0, s1, op0, s2=None, op1=None):
        if op1 is None:
            self.P.op(eng, lambda e: e.tensor_scalar(out=out.ap(), in0=in0.ap(), scalar1=_a(s1), scalar2=None, op0=op0),
                      reads=_ks(in0, s1), writes=_ks(out))
        else:
            self.P.op(eng, lambda e: e.tensor_scalar(out=out.ap(), in0=in0.ap(), scalar1=_a(s1), scalar2=_a(s2), op0=op0, op1=op1),
                      reads=_ks(in0, s1, s2), writes=_ks(out))

    def tt(self, eng, out, in0, in1, op):
        self.P.op(eng, lambda e: e.tensor_tensor(out=out.ap(), in0=in0.ap(), in1=in1.ap(), op=op),
                  reads=_ks(in0, in1), writes=_ks(out))

    def stt(self, out, in0, scalar, in1, op0, op1):
        self.P.op("dve", lambda e: e.scalar_tensor_tensor(out=out.ap(), in0=in0.ap(), scalar=_a(scalar), in1=in1.ap(), op0=op0, op1=op1),
                  reads=_ks(in0, scalar, in1), writes=_ks(out))

    def cp(self, eng, out, in_):
        if eng == "act":
            self.P.op("act", lambda e: e.copy(out=out.ap(), in_=in_.ap()), reads=_ks(in_), writes=_ks(out))
        else:
            self.P.op(eng, lambda e: e.tensor_copy(out=out.ap(), in_=in_.ap()), reads=_ks(in_), writes=_ks(out))

    def memset(self, eng, out, val):
        self.P.op(eng, lambda e: e.memset(out.ap(), val), writes=_ks(out))

    def recip(self, out, in_):
        self.P.op("dve", lambda e: e.reciprocal(out=out.ap(), in_=in_.ap()), reads=_ks(in_), writes=_ks(out))

    def dma(self, q, out, in_):
        self.P.op(q, lambda e: e.dma_start(out=out.ap(), in_=in_.ap()), reads=_ks(in_), writes=_ks(out), dma=True)

    def H(self, kt, t0=0, n=L):
        return self.sb(A_H + kt * L + t0, n)

    def cv(self, name, j=0):
        return self.sb(self.cvoff[name] + j, 1)

    def load_consts(self):
        self.dma("sp", self.sb(A_CONST, NCONST), self.dr("consts", 0, [(NCONST, 128), (1, NCONST)], ["consts"]))
        rows = []
        rows.append(("nm0", "norm_mix", 0, 8)); rows.append(("nm1", "norm_mix", D, 8))
        rows.append(("nf0", "norm_ffn", 0, 8)); rows.append(("nf1", "norm_ffn", D, 8))
        rows.append(("nfin", "norm_final", 0, 8))
        rows.append(("s5D", "s5_D", 0, 4)); rows.append(("glub", "s5_glu_b", 0, 4))
        rows.append(("gam0", "hgrn_gamma", 0, 4)); rows.append(("gam1", "hgrn_gamma", 512, 4))
        rows.append(("hnorm", "hgrn_norm", 0, 4))
        for l in range(2):
            for j in range(3):
                rows.append(("cw%d%d" % (l, j), "ffn_conv_w", (l * 3 + j) * 2 * DFF, 44))
            rows.append(("cb%d" % l, "ffn_conv_b", l * 2 * DFF, 44))
        self.cvoff = {}
        off = A_CV
        groups = []
        cur = []
        n = 0
        for r in rows:
            if n + r[3] > 128:
                groups.append(cur); cur = []; n = 0
            cur.append(r); n += r[3]
        groups.append(cur)
        assert sum(r[3] for r in rows) <= 512
        for gi, grp in enumerate(groups):
            stg = A_Y + (gi % 2) * 128
            p = 0
            for (nm, dn, doff, nr) in grp:
                self.dma("sp", self.sb(stg, 128, p, nr), self.dr(dn, doff, [(128, nr), (1, 128)], [dn]))
                self.cvoff[nm] = off + p
                p += nr
            bank = gi % 2
            self.tr(self.psv(bank, p), self.sb(stg, 128, 0, p))
            self.cp("dve", self.sb(off, p), self.psv(bank, p))
            off += p

    def load_x(self):
        for tt in range(16):
            stg = A_Y + 512 + (tt % 2) * 1024
            self.dma("sp", self.sb(stg, 1024), self.dr("x", tt * 128 * D, [(D, 128), (1, D)], ["x"]))
            for half in range(2):
                bank = (tt * 2 + half) % 4
                for q in range(4):
                    self.tr(self.psv(bank, 128, q * 128), self.sb(stg + (half * 4 + q) * 128, 128))
                eng = "dve" if half == 0 else "act"
                self.cp(eng, self.sbd(A_H + half * 4 * L + tt * 128, [(L, 4), (1, 128)]), self.psd(bank, [(128, 4), (1, 128)]))

    def rmsnorm(self, gname, t0, n, dst_off, dst_stride, tmp_off, bank, bf=False, el=0):
        sq = [tmp_off, tmp_off + 512]
        rs = tmp_off + 1024
        acc = self.psv(bank, n)
        for kt in range(8):
            s = self.sb(sq[kt % 2], n)
            self.act(s, self.H(kt, t0, n), AF.Square)
            self.mm(acc, self.sb(A_CONST + C_ONES, 128), s, start=(kt == 0), stop=(kt == 7))
        rsv = self.sb(rs, n)
        self.act(rsv, acc, AF.Sqrt, scale=1.0 / D, bias=self.sb(A_CONST + C_EPS, 1))
        self.recip(rsv, rsv)
        for kt in range(8):
            dv = self.bf(dst_off, el + kt * dst_stride, n) if bf else self.sb(dst_off + kt * dst_stride, n)
            self.stt(dv, self.H(kt, t0, n), self.cv(gname, kt), rsv, ALU.mult, ALU.mult)

    def load_w(self, dst_off, name, base, row_stride, c0, ncols, nkt=8, q="sp"):
        self.dma(q, self.sbd(dst_off, [(ncols, nkt), (1, ncols)]),
                 self.dr(name, base + c0, [(row_stride, 128), (128 * row_stride, nkt), (1, ncols)], [name]))

    def ffn(self, layer):
        nfn = "nf%d" % layer
        XN = A_X
        G = A_X + 2048
        WIB = [A_X + 7680 + i * 512 for i in range(3)]
        WOB = [A_X + 9216 + i * 1408 for i in range(2)]
        WI = [A_Y + i * 1024 for i in range(3)]
        WO = [A_Y + 3072 + i * 2816 for i in range(2)]
        RAWS = [A_Y + 8704, A_Y + 9728]
        CVA = A_Y + 10752
        CVB = CVA + 512
        SA = CVB + 512
        HALO = A_MISC
        wi_base = layer * D * 2 * DFF
        wo_base = layer * DFF * D
        wcnt = 0
        for b in range(4):
            t0 = b * 512
            self.rmsnorm(nfn, t0, 512, XN, 512, CVA, 7, bf=True)
            for ft in range(22):
                cvs = []
                for ab in range(2):
                    tile = ft + ab * 22
                    wb = WI[wcnt % 3]; wbb = WIB[wcnt % 3]; wcnt += 1
                    self.load_w(wb, "ffn_w_in", wi_base, 2 * DFF, tile * 128, 128)
                    self.cp("act" if ab == 0 else "pool", self.bf(wbb, 0, 1024), self.sb(wb, 1024))
                    bank = (ft * 2 + ab) % 4
                    acc = self.psv(bank)
                    for kt in range(8):
                        self.mm(acc, self.bf(wbb, kt * 128, 128), self.bf(XN, kt * 512, 512), start=(kt == 0), stop=(kt == 7))
                    raw = RAWS[ab]
                    if b == 0:
                        self.memset("pool", self.sb(raw, 2), 0.0)
                    else:
                        self.cp("pool", self.sb(raw, 2), self.sb(HALO + tile * 2, 2))
                    self.cp("act", self.sb(raw + 2, 512), acc)
                    cvt = self.sb(CVA if ab == 0 else CVB, 512)
                    self.act(cvt, acc, AF.Identity, scale=self.cv("cw%d2" % layer, tile), bias=self.cv("cb%d" % layer, tile))
                    self.stt(cvt, self.sb(raw + 1, 512), self.cv("cw%d1" % layer, tile), cvt, ALU.mult, ALU.add)
                    self.stt(cvt, self.sb(raw, 512), self.cv("cw%d0" % layer, tile), cvt, ALU.mult, ALU.add)
                    if b < 3:
                        self.cp("pool", self.sb(HALO + tile * 2, 2), self.sb(raw + 512, 2))
                    cvs.append(cvt)
                sa = self.sb(SA, 512)
                self.act(sa, cvs[0], AF.Silu)
                self.tt("pool", self.bf(G, ft * 512, 512), sa, cvs[1], ALU.mult)
            for dt in range(8):
                wb = WO[dt % 2]; wbb = WOB[dt % 2]
                self.load_w(wb, "ffn_w_out", wo_base, D, dt * 128, 128, nkt=22)
                self.cp("pool" if dt % 2 else "act", self.bf(wbb, 0, 2816), self.sb(wb, 2816))
                acc = self.psv(4 + dt % 2)
                for ft in range(22):
                    self.mm(acc, self.bf(wbb, ft * 128, 128), self.bf(G, ft * 512, 512), start=(ft == 0), stop=(ft == 21))
                hv = self.H(dt, t0, 512)
                self.tt("dve", hv, acc, hv, ALU.add)

    def layer0_mixer(self):
        Y = A_Y
        XN = A_X
        for b in range(4):
            self.rmsnorm("nm0", b * 512, 512, XN, L, Y + 8192, 7, bf=True, el=b * 512)
        LB = A_MISC + 128
        OML = A_MISC + 132
        self.tt("dve", self.sb(LB, 4), self.sb(self.cvoff["gam0"], 4), self.sb(self.cvoff["gam1"], 4), ALU.subtract)
        self.act(self.sb(LB, 4), self.sb(LB, 4), AF.Sigmoid)
        self.ts("dve", self.sb(OML, 4), self.sb(LB, 4), -1.0, ALU.mult, 1.0, ALU.add)
        WA = [Y, Y + 1024]
        WAB = [A_X + 8192, A_X + 8704]
        WVB = A_X + 10240
        TO = [Y + 2048, Y + 2560]
        cnt = 0
        for tile in list(range(0, 12)) + list(range(16, 20)):
            wa = WA[cnt % 2]
            wab = WAB[cnt % 2]
            self.load_w(wa, "mix_w_in", 0, 2560, tile * 128, 128)
            self.cp("pool", self.bf(wab, 0, 1024), self.sb(wa, 1024))
            for b in range(4):
                acc = self.psv(cnt % 2 * 2 + b % 2)
                for kt in range(8):
                    self.mm(acc, self.bf(wab, kt * 128, 128), self.bf(XN, kt * L + b * 512, 512), start=(kt == 0), stop=(kt == 7))
                to = self.sb(TO[b % 2], 512)
                if tile < 4:
                    self.cp("dve", to, acc)
                elif tile < 8 or tile >= 16:
                    self.act(to, acc, AF.Silu)
                else:
                    self.act(to, acc, AF.Sigmoid)
                    self.ts("dve", to, to, self.sb(OML + tile - 8, 1), ALU.mult, self.sb(LB + tile - 8, 1), ALU.add)
                self.dma("pool", self.dr("PJ", tile * 128 * L + b * 512, [(L, 128), (1, 512)], [("PJ", tile, b)]), to)
            cnt += 1
        WV = Y + 4096
        self.load_w(WV, "mix_w_in", 0, 2560, 1536, 512)
        self.cp("act", self.bf(WVB, 0, 2048), self.sb(WV, 2048))
        self.cp("pool", self.bf(WVB, 2048, 2048), self.sb(WV + 2048, 2048))
        for tt_ in range(16):
            pv = self.psv(4 + tt_ % 2)
            for kt in range(8):
                self.mm(pv, self.bf(XN, kt * L + tt_ * 128, 128), self.bf(WVB, kt * 512, 512), start=(kt == 0), stop=(kt == 7))
            vo = self.sb(TO[tt_ % 2], 512)
            self.cp("act" if tt_ % 2 else "dve", vo, pv)
            self.dma("pool", self.dr("VT", tt_ * 128 * 512, [(512, 128), (1, 512)], [("VT", tt_)]), vo)
        if self.mode != "l0b":
            self.hgrn2()
        if self.mode != "l0a":
            self.s5()
        WO = [Y, Y + 1024]
        for dt in range(8):
            wo = WO[dt % 2]
            self.load_w(wo, "mix_w_out", 0, D, dt * 128, 128)
            for b in range(4):
                acc = self.psv((dt * 4 + b) % 4)
                for kt in range(8):
                    self.mm(acc, self.sb(wo + kt * 128, 128), self.sb(A_X + kt * L + b * 512, 512), start=(kt == 0), stop=(kt == 7))
                hv = self.H(dt, b * 512, 512)
                self.tt("dve", hv, acc, hv, ALU.add)
        if self.dbg:
            for kt in range(8):
                self.dma("pool", self.dr("QK", kt * 128 * L, [(L, 128), (1, L)], [("QKdbg", kt)]), self.sb(A_X + kt * L, L))

    def pj_load(self, dst, tile):
        self.dma("sp", self.sb(dst, L), self.dr("PJ", tile * 128 * L, [(L, 128), (1, L)], [("PJ", tile, b) for b in range(4)]))

    def hgrn2(self):
        Y = A_Y
        RM = A_X
        QT = A_X + 2048
        KT = A_X + 4096
        BB = A_X + 6144
        ENB = Y
        KTOK = Y + 2048
        VTOK = Y + 4096
        OH = Y + 6144
        SG = Y + 8192
        SS = [Y + 10240, Y + 10368]
        ATT = Y + 10496
        TMP = Y + 10752
        self.memset("pool", self.sb(RM, L), 1.0)
        self.memset("pool", self.sbd(RM, [(64, 32)]), 0.0)
        for h in range(4):
            self.pj_load(QT, 4 + h)
            self.pj_load(BB, 8 + h)
            self.pj_load(SG, 16 + h)
            self.dma("sp", self.sbd(VTOK, [(128, 16), (1, 128)]),
                     self.dr("VT", h * 128, [(512, 128), (128 * 512, 16), (1, 128)], [("VT", t) for t in range(16)]))
            f = self.sb(BB, L)
            kk = self.sb(KT, L)
            self.ts("dve", kk, f, -1.0, ALU.mult, 1.0, ALU.add)
            self.act(f, f, AF.Ln)
            self.P.op("dve", lambda e, o=self.sb(BB, L), d0=self.sb(RM, L): e.tensor_tensor_scan(
                out=o.ap(), data0=d0.ap(), data1=o.ap(), initial=0.0, op0=ALU.mult, op1=ALU.add),
                reads=_ks(self.sb(BB, L), self.sb(RM, L)), writes=_ks(self.sb(BB, L)))
            self.act(self.sb(ENB, L), f, AF.Exp, scale=-1.0)
            self.act(f, f, AF.Exp)
            self.tt("dve", self.sb(QT, L), self.sb(QT, L), f, ALU.mult)
            self.tt("pool", kk, kk, self.sb(ENB, L), ALU.mult)
            for t_ in range(16):
                pb = self.psv(t_ % 2, 128)
                self.tr(pb, self.sb(KT + t_ * 128, 128))
                self.cp("act" if t_ % 2 else "dve", self.sb(KTOK + t_ * 128, 128), pb)
            for c in range(32):
                t_, half = c // 2, c % 2
                p0 = 64 * half
                c0 = c * 64
                pa = self.psv(2 + c % 2, 64, 0, p0, 64)
                self.mm(pa, self.sb(KT + c0, 64), self.sb(QT + c0, 64))
                att = self.sb(ATT + (c % 2) * 64, 64, p0, 64)
                self.tt("dve", att, pa, self.sb(A_CONST + C_HGM, 64, p0, 64), ALU.mult)
                po = self.psv(4 + c % 2, 64)
                vt = self.sb(VTOK + t_ * 128, 128, p0, 64)
                S_old = self.sb(SS[(c + 1) % 2], 128)
                S_new = self.sb(SS[c % 2], 128)
                self.mm(po, vt, att, start=True, stop=(c == 0))
                if c > 0:
                    self.mm(po, S_old, self.sb(QT + c0, 64), start=False, stop=True)
                self.cp("act", self.sb(OH + c0, 64), po)
                if c < 31:
                    pS = self.psv(6 + c % 2, 128)
                    self.mm(pS, self.sb(KTOK + t_ * 128, 128, p0, 64), vt, start=True, stop=(c == 0))
                    if c > 0:
                        self.mm(pS, self.sb(A_CONST + C_IDENT, 128), S_old, start=False, stop=True)
                    self.ts("dve", S_new, pS, self.sb(BB + c0 + 63, 1), ALU.mult)
            for b in range(4):
                sq = self.sb(TMP, 512)
                self.act(sq, self.sb(OH + b * 512, 512), AF.Square)
                pn = self.psv(b % 2)
                self.mm(pn, self.sb(A_CONST + C_ONES, 128), sq)
                rs = self.sb(TMP + 512, 512)
                self.act(rs, pn, AF.Sqrt, scale=1.0 / 128, bias=self.sb(A_CONST + C_EPS, 1))
                self.recip(rs, rs)
                ob = self.sb(A_X + (4 + h) * L + b * 512, 512)
                self.stt(ob, self.sb(OH + b * 512, 512), self.cv("hnorm", h), rs, ALU.mult, ALU.mult)
                self.tt("pool", ob, ob, self.sb(SG + b * 512, 512), ALU.mult)

    def s5(self):
        Y = A_Y
        UT = Y
        PP = [(Y + 2048, Y + 4096), (Y + 6144, Y + 8192)]
        SM = Y + 10240
        EB = [SM, SM + 128]
        BD = [SM + 256, SM + 384]
        EC = [SM + 512, SM + 640]
        BT = SM + 768
        SC = SM + 896
        GT = SM + 1152
        it16 = V(self.it, "it", 2048, 0, [(1, 16)])

        def g(i):
            return self.sb(GT + 16 * i, 16)
        ARE, AIM, DT, AR, AI, CRE, CIM, T0, T1, T2 = [g(i) for i in range(10)]
        PWB = GT + 160

        def pw(k, j):
            return self.sb(PWB + (k * 3 + j) * 16, 16)
        for name, dst in (("s5_A_re", ARE), ("s5_A_im", AIM)):
            st = self.sb(SC, 128, 0, 16)
            self.dma("sp", st, self.dr(name, 0, [(128, 16), (1, 128)], [name]))
            pb = self.psv(0, 16)
            self.tr(pb, st)
            self.cp("dve", dst, pb)
        ld = self.sb(BT, 32)
        self.dma("sp", ld, self.dr("s5_log_dt", 0, [(0, 128), (1, 32)], ["s5_log_dt"]))
        self.cp("dve", self.sb(GT + 32, 16, 0, 64), self.sbd(BT, [(2, 16)], 0, 64))
        self.cp("dve", self.sb(GT + 32, 16, 64, 64), self.sbd(BT + 1, [(2, 16)], 64, 64))
        pb = DT
        self.act(DT, pb, AF.Exp)
        self.tt("dve", T0, ARE, DT, ALU.mult)
        self.act(T0, T0, AF.Exp)
        self.tt("dve", T1, AIM, DT, ALU.mult)
        self.ts("dve", T1, T1, 1.0 / (2 * np.pi), ALU.mult)
        for dst, shift in ((AI, 0.0), (AR, 0.25)):
            if shift:
                self.ts("dve", T1, T1, shift, ALU.add)
            self.cp("dve", it16, T1)
            self.cp("dve", T2, it16)
            self.tt("dve", T2, T1, T2, ALU.subtract)
            self.act(dst, T2, AF.Sin, scale=float(2 * np.pi))
        self.tt("dve", AR, AR, T0, ALU.mult)
        self.tt("dve", AI, AI, T0, ALU.mult)
        self.tt("dve", T0, ARE, ARE, ALU.mult)
        self.tt("dve", T1, AIM, AIM, ALU.mult)
        self.tt("dve", T0, T0, T1, ALU.add)
        self.recip(T0, T0)
        self.ts("dve", T1, AR, -1.0, ALU.add)
        self.tt("dve", CRE, T1, ARE, ALU.mult)
        self.tt("dve", T2, AI, AIM, ALU.mult)
        self.tt("dve", CRE, CRE, T2, ALU.add)
        self.tt("dve", CRE, CRE, T0, ALU.mult)
        self.tt("dve", CIM, AI, ARE, ALU.mult)
        self.tt("dve", T2, T1, AIM, ALU.mult)
        self.tt("dve", CIM, CIM, T2, ALU.subtract)
        self.tt("dve", CIM, CIM, T0, ALU.mult)
        self.cp("dve", pw(0, 0), AR)
        self.cp("dve", pw(0, 1), AI)
        for k in range(11):
            if k > 0:
                self.tt("dve", T0, pw(k - 1, 0), pw(k - 1, 0), ALU.mult)
                self.tt("dve", T1, pw(k - 1, 1), pw(k - 1, 1), ALU.mult)
                self.tt("dve", pw(k, 0), T0, T1, ALU.subtract)
                self.tt("dve", T0, pw(k - 1, 0), pw(k - 1, 1), ALU.mult)
                self.ts("dve", pw(k, 1), T0, 2.0, ALU.mult)
            self.ts("dve", pw(k, 2), pw(k, 1), -1.0, ALU.mult)
        for q in range(16):
            ct, gq = q // 4, q % 4
            if gq == 0:
                self.pj_load(UT, ct)
                for b in range(4):
                    self.ts("pool", self.sb(A_X + ct * L + b * 512, 512), self.sb(UT + b * 512, 512), self.cv("s5D", ct), ALU.mult)
            btr, bti, bbr, bbi, btmp = [self.sb(BT + 16 * i, 16) for i in range(5)]
            self.dma("sp", btr, self.dr("s5_B_re", q * 2048, [(16, 128), (1, 16)], ["s5_B_re"]))
            self.dma("sp", bti, self.dr("s5_B_im", q * 2048, [(16, 128), (1, 16)], ["s5_B_im"]))
            cre, cim = self.sb(GT + 16 * 5 + q, 1), self.sb(GT + 16 * 6 + q, 1)
            self.ts("dve", btmp, bti, cim, ALU.mult)
            self.stt(bbr, btr, cre, btmp, ALU.mult, ALU.subtract)
            self.ts("dve", btmp, btr, cim, ALU.mult)
            self.stt(bbi, bti, cre, btmp, ALU.mult, ALU.add)
            for part, bb in ((0, bbr), (1, bbi)):
                self.memset("pool", self.sb(EB[part], 128), 0.0)
                for g2 in range(2):
                    self.cp("pool", self.sb(EB[part] + 32 * gq + 16 * g2, 16, 64 * g2, 64), self.sb(BT + 16 * (2 + part), 16, 64 * g2, 64))
                pb = self.psv(2 + part, 128)
                self.tr(pb, self.sb(EB[part], 128))
                self.cp("act", self.sb(BD[part], 128), pb)
            for part, name in ((0, "s5_C_re"), (1, "s5_C_im")):
                st = self.sb(SC + 128 * part, 128, 0, 16)
                self.dma("sp", self.sbd(SC + 128 * part, [(64, 2), (1, 64)], 0, 16), self.dr(name, q * 2048, [(64, 16), (1024, 2), (1, 64)], [name]))
                pb = self.psv(4 + part, 16)
                self.tr(pb, st)
                self.memset("pool", self.sb(EC[part], 128), 0.0)
                for g2 in range(2):
                    dst = self.sb(EC[part] + 32 * gq + 16 * g2, 16, 64 * g2, 64)
                    src = self.psv(4 + part, 16, 0, 64 * g2, 64)
                    if part == 0:
                        self.cp("dve", dst, src)
                    else:
                        self.ts("dve", dst, src, -1.0, ALU.mult)
            cur = 0
            for part in range(2):
                for b in range(4):
                    pb = self.psv(6 + b % 2)
                    self.mm(pb, self.sb(BD[part], 128), self.sb(UT + b * 512, 512))
                    self.cp("act", self.sb(PP[0][part] + b * 512, 512), pb)
            for k in range(11):
                sft = 1 << k
                n = L - sft
                a_re, a_im = PP[cur]
                b_re, b_im = PP[1 - cur]
                sar = self.sb(PWB + (k * 3 + 0) * 16 + q, 1)
                sai = self.sb(PWB + (k * 3 + 1) * 16 + q, 1)
                snai = self.sb(PWB + (k * 3 + 2) * 16 + q, 1)
                self.stt(self.sb(b_re + sft, n), self.sb(a_re, n), sar, self.sb(a_re + sft, n), ALU.mult, ALU.add)
                self.stt(self.sb(b_re + sft, n), self.sb(a_im, n), snai, self.sb(b_re + sft, n), ALU.mult, ALU.add)
                self.stt(self.sb(b_im + sft, n), self.sb(a_im, n), sar, self.sb(a_im + sft, n), ALU.mult, ALU.add)
                self.stt(self.sb(b_im + sft, n), self.sb(a_re, n), sai, self.sb(b_im + sft, n), ALU.mult, ALU.add)
                self.cp("pool", self.sb(b_re, sft), self.sb(a_re, sft))
                self.cp("pool", self.sb(b_im, sft), self.sb(a_im, sft))
                cur = 1 - cur
            x_re, x_im = PP[cur]
            for b in range(4):
                pb = self.psv(b % 2)
                self.mm(pb, self.sb(EC[0], 128), self.sb(x_re + b * 512, 512), start=True, stop=False)
                self.mm(pb, self.sb(EC[1], 128), self.sb(x_im + b * 512, 512), start=False, stop=True)
                yv = self.sb(A_X + ct * L + b * 512, 512)
                self.tt("dve", yv, pb, yv, ALU.add)
        TG = Y
        for ct in range(4):
            yv = self.sb(A_X + ct * L, L)
            t = self.sb(TG + (ct % 2) * 2048, L)
            self.act(t, yv, AF.Square)
            self.ts("dve", t, t, 0.044715, ALU.mult, 1.0, ALU.add)
            self.tt("pool", t, t, yv, ALU.mult)
            self.act(t, t, AF.Sigmoid, scale=1.5957691216057308)
            self.tt("dve", yv, yv, t, ALU.mult)
        GW = SM
        GG = Y
        for ot in range(4):
            self.dma("sp", self.sbd(GW, [(128, 4), (1, 128)]), self.dr("s5_glu_w", ot * 128, [(512, 128), (128 * 512, 4), (1, 128)], ["s5_glu_w"]))
            for b in range(4):
                pb = self.psv(2 + b % 2)
                for kt in range(4):
                    self.mm(pb, self.sb(GW + kt * 128, 128), self.sb(A_X + kt * L + b * 512, 512), start=(kt == 0), stop=(kt == 3))
                self.act(self.sb(GG + ot * L + b * 512, 512), pb, AF.Sigmoid, bias=self.cv("glub", ot))
        for ct in range(4):
            yv = self.sb(A_X + ct * L, L)
            self.tt("dve" if ct % 2 else "pool", yv, yv, self.sb(GG + ct * L, L), ALU.mult)

    def rot_tables(self, COS, SIN, R, NF):
        it = V(self.it, "it", 2048, 0, [(1, L)])
        self.dma("sp", it, self.dr("positions", 0, [(0, 128), (1, L)], ["positions"]))
        r = self.sb(R, L)
        nf = self.sb(NF, L)
        self.cp("dve", r, it)
        self.ts("dve", r, r, self.sb(A_CONST + C_INVF, 1), ALU.mult, 1.0 / (2 * np.pi), ALU.mult)
        for dst, shift in ((SIN, 0.0), (COS, 0.25)):
            if shift:
                self.ts("dve", r, r, shift, ALU.add)
            self.cp("dve", it, r)
            self.cp("dve", nf, it)
            self.tt("dve", nf, r, nf, ALU.subtract)
            self.act(self.sb(dst, L), nf, AF.Sin, scale=float(2 * np.pi))
        self.ts("dve", self.sb(SIN, L), self.sb(SIN, L), self.sb(A_CONST + C_SGN, 1), ALU.mult)

    def layer1_mixer(self):
        Y = A_Y
        COS, SIN = Y, Y + 2048
        WA = [Y + 4096, Y + 5120]
        WS = [Y + 6144, Y + 7168]
        T1 = [Y + 8192, Y + 8704]
        T2 = [Y + 9216, Y + 9728]
        QO = [Y + 10240, Y + 10752]
        XN = A_X
        for b in range(4):
            self.rmsnorm("nm1", b * 512, 512, XN, L, T1[0], 7, bf=True, el=b * 512)
        self.rot_tables(COS, SIN, WA[0], WS[0])
        WAB = [A_X + 8192, A_X + 8704]
        WSB = [A_X + 9216, A_X + 9728]
        WVB = A_X + 10240
        for w in WSB:
            self.memset("pool", self.sb(w, 512), 0.0)
        cnt = 0
        for i in range(24):
            wa, wab, wsb = WA[i % 2], WAB[i % 2], WSB[i % 2]
            self.load_w(wa, "att_w_qkv", 0, 4608, i * 128, 128)
            self.cp("act", self.bf(wab, 0, 1024), self.sb(wa, 1024))
            for lo, hi in ((0, 8), (8, 0)):
                self.cp("pool", self.bfd(wsb, lo, [(128, 8), (64, 2), (1, 8)]), self.sbd(wa + hi, [(128, 8), (64, 2), (1, 8)]))
            for b in range(4):
                pa = self.psv(cnt % 2)
                pb = self.psv(2 + cnt % 2)
                for kt in range(8):
                    self.mm(pa, self.bf(wab, kt * 128, 128), self.bf(XN, kt * L + b * 512, 512), start=(kt == 0), stop=(kt == 7))
                for kt in range(8):
                    self.mm(pb, self.bf(wsb, kt * 128, 128), self.bf(XN, kt * L + b * 512, 512), start=(kt == 0), stop=(kt == 7))
                t1 = self.sb(T1[cnt % 2], 512)
                t2 = self.sb(T2[cnt % 2], 512)
                qo = self.sb(QO[cnt % 2], 512)
                self.tt("dve", t1, pa, self.sb(COS + b * 512, 512), ALU.mult)
                self.tt("dve", t2, pb, self.sb(SIN + b * 512, 512), ALU.mult)
                self.tt("pool", qo, t1, t2, ALU.add)
                self.dma("pool", self.dr("QK", i * 128 * L + b * 512, [(L, 128), (1, 512)], [("QK", i, b)]), qo)
                cnt += 1
        WV = Y + 4096
        for g in range(3):
            dil = (1, 4, 16)[g]
            nbk = 16 // dil
            self.load_w(WV, "att_w_qkv", 0, 4608, 3072 + g * 512, 512)
            self.cp("act", self.bf(WVB, 0, 2048), self.sb(WV, 2048))
            self.cp("pool", self.bf(WVB, 2048, 2048), self.sb(WV + 2048, 2048))
            for blk in range(16):
                r, n = blk // nbk, blk % nbk
                base = r + dil * 128 * n
                pv = self.psv(4 + blk % 2)
                for kt in range(8):
                    self.mm(pv, self.bfd(XN, kt * L + base, [(dil, 128)]), self.bf(WVB, kt * 512, 512), start=(kt == 0), stop=(kt == 7))
                vo = self.sb(T1[0] + (blk % 2) * 1024, 512)
                self.cp("act" if blk % 2 else "dve", vo, pv)
                self.dma("pool", self.dr("VS", (g * 16 + blk) * 128 * 512, [(512, 128), (1, 512)], [("VS", g, blk)]), vo)
        OT = Y
        PT = [Y + 8192, Y + 8704, Y + 9216]
        RD = Y + 9728
        WO = Y + 11776
        ACC_O = A_X + 12288
        ACC_D = A_X + 14336
        ones64 = self.sb(A_CONST + C_ONES, 64)
        ucnt = 0
        pcnt = 0
        ocnt = 0
        for j in range(4):
            for g in range(3):
                dil = (1, 4, 16)[g]
                nbk = 16 // dil
                base_x = A_X + (ucnt % 2) * 6144
                ucnt += 1
                QT, KT, VB = base_x, base_x + 2048, base_x + 4096
                self.dma("sp", self.sb(QT, L), self.dr("QK", (g * 4 + j) * 128 * L, [(L, 128), (1, L)], [("QK", g * 4 + j, b) for b in range(4)]))
                self.dma("sp", self.sb(KT, L), self.dr("QK", (12 + g * 4 + j) * 128 * L, [(L, 128), (1, L)], [("QK", 12 + g * 4 + j, b) for b in range(4)]))
                self.dma("sp", self.sbd(VB, [(128, 16), (1, 128)]),
                         self.dr("VS", g * 16 * 128 * 512 + j * 128, [(512, 128), (128 * 512, 16), (1, 128)], [("VS", g, blk) for blk in range(16)]))
                for hh in range(2):
                    p0 = 64 * hh
                    for r in range(dil):
                        prev = None
                        for n in range(nbk):
                            blk = r * nbk + n
                            base = r + dil * 128 * n
                            nq = 256 if n + 1 < nbk else 128
                            st = self.psv(pcnt % 3, nq)
                            self.mm(st, self.sbd(KT + base, [(dil, 128)], p0, 64), self.sbd(QT + base, [(dil, nq)], p0, 64))
                            pt = self.sb(PT[pcnt % 3], nq)
                            self.act(pt, st, AF.Exp, scale=0.125)
                            self.tt("pool" if pcnt % 2 else "dve", pt, pt, self.sb(A_CONST + C_ATTM, nq), ALU.mult)
                            po = self.psv(4 + ocnt % 2, 128, 0, p0, 64)
                            pd = self.psv(6 + ocnt % 2, 128, 0, p0, 64)
                            ocnt += 1
                            vcur = self.sb(VB + blk * 128 + hh * 64, 64)
                            if prev is not None:
                                vprev = self.sb(VB + (blk - 1) * 128 + hh * 64, 64)
                                pprev = self.sb(prev + 128, 128)
                                self.mm(po, vprev, pprev, start=True, stop=False)
                                self.mm(po, vcur, self.sb(PT[pcnt % 3], 128), start=False, stop=True)
                                self.mm(pd, ones64, pprev, start=True, stop=False)
                                self.mm(pd, ones64, self.sb(PT[pcnt % 3], 128), start=False, stop=True)
                            else:
                                self.mm(po, vcur, self.sb(PT[pcnt % 3], 128), start=True, stop=True)
                                self.mm(pd, ones64, self.sb(PT[pcnt % 3], 128), start=True, stop=True)
                            ao = self.sbd(ACC_O + base, [(dil, 128)], p0, 64)
                            ad = self.sbd(ACC_D + base, [(dil, 128)], p0, 64)
                            if g == 0:
                                self.cp("act", ao, po)
                                self.cp("act", ad, pd)
                            else:
                                self.tt("dve", ao, po, ao, ALU.add)
                                self.tt("dve", ad, pd, ad, ALU.add)
                            prev = PT[pcnt % 3]
                            pcnt += 1
            rd = self.sb(RD, L)
            self.recip(rd, self.sb(ACC_D, L))
            self.tt("dve", self.sb(OT + j * L, L), self.sb(ACC_O, L), rd, ALU.mult)
        for dt in range(8):
            self.dma("sp", self.sbd(WO, [(128, 4), (1, 128)]), self.dr("att_w_o", dt * 128, [(D, 128), (128 * D, 4), (1, 128)], ["att_w_o"]))
            for b in range(4):
                acc = self.psv((dt * 4 + b) % 4)
                for j in range(4):
                    self.mm(acc, self.sb(WO + j * 128, 128), self.sb(OT + j * L + b * 512, 512), start=(j == 0), stop=(j == 3))
                hv = self.H(dt, b * 512, 512)
                self.tt("dve", hv, acc, hv, ALU.add)

    def final(self):
        XN = A_X
        for b in range(4):
            t0 = b * 512
            self.rmsnorm("nfin", t0, 512, XN, 512, A_Y, 7)
            for ts_ in range(4):
                stg = A_Y + 2048 + (ts_ % 2) * 1024
                for half in range(2):
                    bank = (ts_ * 2 + half) % 4
                    for q in range(4):
                        kt = half * 4 + q
                        self.tr(self.psv(bank, 128, q * 128), self.sb(XN + kt * 512 + ts_ * 128, 128))
                    self.cp("dve" if half == 0 else "act", self.sb(stg + half * 512, 512), self.psv(bank))
                tt = b * 4 + ts_
                self.dma("pool", self.dr("out", tt * 128 * D, [(D, 128), (1, D)], ["out%d" % tt]), self.sb(stg, 1024))

    def build(self):
        nc = self.nc
        with ExitStack() as es:
            self.ar = es.enter_context(nc.sbuf_tensor("arena", [128, ARENA], F32))
            self.ps = es.enter_context(nc.psum_tensor("psum", [128, 4096], F32))
            self.arb = self.ar.bitcast(BF16)
            es.enter_context(nc.allow_low_precision("bf16 matmul operands (fp32 PSUM accumulation) in the dense projections"))
            self.it = es.enter_context(nc.sbuf_tensor("itile", [128, 2048], I32))
            self.load_consts()
            self.load_x()
            m = self.mode
            if m in ("full", "l0", "l0a", "l0b"):
                self.layer0_mixer()
            if m in ("full", "l0", "ffn0") and m not in ("l0a", "l0b"):
                self.ffn(0)
            if m in ("full", "l1", "l1y"):
                self.layer1_mixer()
            if m == "l1y":
                self.memset("pool", self.sb(A_Y, 16), 0.0)
            if m in ("full", "l1", "ffn1", "l1y"):
                self.ffn(1)
            self.final()
            P = self.P
            P.emit(lambda name: es.enter_context(nc.semaphore(name)))
            with nc.Block() as block:
                @block.sync
                def _(e):
                    P.run_engine("sp", e)

                @block.tensor
                def _(e):
                    P.run_engine("pe", e)

                @block.scalar
                def _(e):
                    P.run_engine("act", e)

                @block.vector
                def _(e):
                    P.run_engine("dve", e)

                @block.gpsimd
                def _(e):
                    P.run_engine("pool", e)
        return nc


def kernel(**inputs):
    mode = inputs.pop("_mode", "full")
    dbg = inputs.pop("_dbg", False)
    ncores = inputs.pop("_ncores", 8)
    kb = KB(mode, dbg)
    nc = kb.build()
    consts = _const_table()
    in_maps = []
    for c in range(ncores):
        m = {}
        for n, shp, dt in IN_SPECS:
            if n == "consts":
                m[n] = consts
            elif n == "x":
                m[n] = np.ascontiguousarray(inputs["x"][c], dtype=np.float32)
            elif n == "positions":
                m[n] = np.ascontiguousarray(inputs["positions"][c], dtype=np.int32)
            else:
                m[n] = np.ascontiguousarray(inputs[n])
        in_maps.append(m)
    if inputs.get("_trace"):
        res = run_bass_kernel_spmd(nc, in_maps, core_ids=list(range(ncores)), trace=True)
        print("EXEC_TIME_NS", res.exec_time_ns)
    else:
        res = run_bass_kernel_spmd(nc, in_maps, core_ids=list(range(ncores)))
    if dbg:
        return res.results
    return np.stack([r["out"] for r in res.results], axis=0).astype(np.float32)
```
